# Optimizing a Trainium2 kernel written in Bass

```python
import math
import jax, jax.numpy as jnp
from jax import lax
import numpy as np

D_MODEL = 1024
BATCH = 4
SEQ = 4096
DEPTH = 2

HEAD_DIM = 64
BRANCH_WIDTH = D_MODEL // 2
N_BRANCHES = 4
Q_BLOCK = 128
FOX_HEADS = BRANCH_WIDTH // HEAD_DIM
LRU_BLOCKS = 8
LRU_BLOCK_DIM = BRANCH_WIDTH // LRU_BLOCKS
CONV_WIDTH = 4
LRU_C = 8.0
NSA_HEADS = BRANCH_WIDTH // HEAD_DIM
NSA_KV_HEADS = 2
NSA_GROUP = NSA_HEADS // NSA_KV_HEADS
NSA_KV_WIDTH = NSA_KV_HEADS * HEAD_DIM
CMP_BLOCK = 32
CMP_STRIDE = 16
CMP_HIDDEN = 2 * HEAD_DIM
SLC_BLOCK = 64
SLC_TOPK = 16
SLC_LOCAL = 2
SLC_FORCE_SCORE = 1e6
WINDOW = 512
SG_CHUNK = 128
SG_GROUPS = 8
SG_GROUP_DIM = BRANCH_WIDTH // SG_GROUPS

NORM_EPS = 1e-6
NEG_INF = -1e30

kernel_name = "hybrid_fox_rglru_nsa_sgmlp_block"


def _in_split_sizes():
    W = BRANCH_WIDTH
    return (W, W, W, FOX_HEADS, W,
            W, W,
            W, NSA_KV_WIDTH, NSA_KV_WIDTH, NSA_KV_WIDTH,
            NSA_KV_WIDTH, NSA_KV_WIDTH, NSA_KV_WIDTH,
            3 * NSA_HEADS, W,
            W, W, W,
            N_BRANCHES * D_MODEL)


def rms_norm(x, g):
    x32 = x.astype(jnp.float32)
    y = x32 * lax.rsqrt(jnp.mean(x32 * x32, axis=-1, keepdims=True) + NORM_EPS)
    return (y * g.astype(jnp.float32)).astype(x.dtype)


def layer_norm(x, g):
    x32 = x.astype(jnp.float32)
    mu = jnp.mean(x32, axis=-1, keepdims=True)
    xc = x32 - mu
    y = xc * lax.rsqrt(jnp.mean(xc * xc, axis=-1, keepdims=True) + NORM_EPS)
    return (y * g.astype(jnp.float32)).astype(x.dtype)


def masked_softmax(s, mask):
    p = jax.nn.softmax(jnp.where(mask, s, NEG_INF), axis=-1)
    return jnp.where(mask, p, 0.0)


def fox_attention(q, k, v, f_logit):
    B, S, H, dh = q.shape
    n_blk = S // Q_BLOCK
    scale = 1.0 / math.sqrt(dh)
    c = jnp.cumsum(jax.nn.log_sigmoid(f_logit.astype(jnp.float32)), axis=1)
    c_k = c.transpose(0, 2, 1)[:, :, None, :]
    qb = q.reshape(B, n_blk, Q_BLOCK, H, dh).transpose(1, 0, 2, 3, 4)
    cb = c.reshape(B, n_blk, Q_BLOCK, H).transpose(1, 0, 2, 3)
    pos_k = jnp.arange(S)

    def block(args):
        q_i, c_i, i = args
        t = i * Q_BLOCK + jnp.arange(Q_BLOCK)
        s = jnp.einsum('bqhd,bkhd->bhqk', q_i, k).astype(jnp.float32) * scale
        s = s + c_i.transpose(0, 2, 1)[..., None] - c_k
        p = masked_softmax(s, pos_k[None, :] <= t[:, None])
        return jnp.einsum('bhqk,bkhd->bqhd', p.astype(v.dtype), v)

    out = lax.map(block, (qb, cb, jnp.arange(n_blk)))
    return out.transpose(1, 0, 2, 3, 4).reshape(B, S, H, dh)


def rg_lru(xb, conv_w, conv_b, w_a, b_a, w_x, b_x, lam):
    B, S, W = xb.shape
    xc = lax.conv_general_dilated(
        xb, conv_w.reshape(CONV_WIDTH, 1, W).astype(xb.dtype), window_strides=(1,),
        padding=[(CONV_WIDTH - 1, 0)], dimension_numbers=('NWC', 'WIO', 'NWC'),
        feature_group_count=W) + conv_b
    xh = xc.reshape(B, S, LRU_BLOCKS, LRU_BLOCK_DIM)
    r = jax.nn.sigmoid((jnp.einsum('bsnd,nde->bsne', xh, w_a).reshape(B, S, W) + b_a).astype(jnp.float32))
    i_g = jax.nn.sigmoid((jnp.einsum('bsnd,nde->bsne', xh, w_x).reshape(B, S, W) + b_x).astype(jnp.float32))
    log_a = -LRU_C * r * jax.nn.softplus(-lam.astype(jnp.float32))
    a = jnp.exp(log_a)
    b = jnp.sqrt(-jnp.expm1(2.0 * log_a)) * (i_g * xc.astype(jnp.float32))

    def combine(left, right):
        a_l, b_l = left
        a_r, b_r = right
        return a_l * a_r, a_r * b_l + b_r

    _, h = lax.associative_scan(combine, (a, b), axis=1)
    return h.astype(xb.dtype)


def _cmp_slc_overlap(n_cmp, n_slc):
    c0 = np.arange(n_cmp) * CMP_STRIDE
    s0 = np.arange(n_slc) * SLC_BLOCK
    ov = np.minimum(c0[:, None] + CMP_BLOCK, s0[None, :] + SLC_BLOCK) - np.maximum(c0[:, None], s0[None, :])
    return (np.clip(ov, 0, None) / CMP_BLOCK).astype(np.float32)


def nsa_attention(q, k_cmp, v_cmp, k_slc, v_slc, k_win, v_win, gates,
                  kn_g, cmp_pos, wk1, wk2, wv1, wv2):
    B, S, H, dh = q.shape
    G, R = NSA_KV_HEADS, NSA_GROUP
    scale = 1.0 / math.sqrt(dh)
    n_cmp = (S - CMP_BLOCK) // CMP_STRIDE + 1
    blk_idx = np.arange(n_cmp)[:, None] * CMP_STRIDE + np.arange(CMP_BLOCK)[None, :]
    cmp_end = jnp.asarray(blk_idx[:, -1])

    def compress(t, w1, w2):
        tb = t[:, blk_idx] + cmp_pos[None, None, :, None, :]
        tb = tb.transpose(0, 1, 3, 2, 4).reshape(B, n_cmp, G, CMP_BLOCK * dh)
        return jax.nn.silu(tb @ w1) @ w2

    k_c = rms_norm(compress(k_cmp, wk1, wk2), kn_g)
    v_c = compress(v_cmp, wv1, wv2)
    n_slc = S // SLC_BLOCK
    top_k = min(SLC_TOPK, n_slc)
    overlap = jnp.asarray(_cmp_slc_overlap(n_cmp, n_slc))
    ks_blk = rms_norm(k_slc, kn_g).reshape(B, n_slc, SLC_BLOCK, G, dh).transpose(0, 3, 1, 2, 4)
    vs_blk = v_slc.reshape(B, n_slc, SLC_BLOCK, G, dh).transpose(0, 3, 1, 2, 4)
    kw_pad = jnp.pad(rms_norm(k_win, kn_g), ((0, 0), (WINDOW, 0), (0, 0), (0, 0)))
    vw_pad = jnp.pad(v_win, ((0, 0), (WINDOW, 0), (0, 0), (0, 0)))
    span = WINDOW + Q_BLOCK

    n_blk = S // Q_BLOCK
    qg = q.reshape(B, n_blk, Q_BLOCK, G, R, dh).transpose(1, 0, 2, 3, 4, 5)
    gb = gates.reshape(B, n_blk, Q_BLOCK, 3, H).transpose(1, 0, 2, 3, 4)
    b_ix = jnp.arange(B)[:, None, None, None]
    g_ix = jnp.arange(G)[None, :, None, None]
    j = jnp.arange(n_slc)

    def block(args):
        q_i, g_i, i = args
        t = i * Q_BLOCK + jnp.arange(Q_BLOCK)
        s_c = jnp.einsum('bqgrd,bcgd->bgrqc', q_i, k_c).astype(jnp.float32) * scale
        p_c = masked_softmax(s_c, cmp_end[None, :] <= t[:, None])
        o_c = jnp.einsum('bgrqc,bcgd->bqgrd', p_c.astype(v_c.dtype), v_c)
        imp = jnp.einsum('bgrqc,cj->bgqj', p_c, overlap)
        jt = t // SLC_BLOCK
        valid = j[None, :] <= jt[:, None]
        forced = (j[None, :] == 0) | (valid & (j[None, :] > jt[:, None] - SLC_LOCAL))
        score = jnp.where(forced, SLC_FORCE_SCORE, jnp.where(valid, imp, -1.0))
        top_val, top_idx = lax.top_k(score, top_k)
        k_sel = ks_blk[b_ix, g_ix, top_idx]
        v_sel = vs_blk[b_ix, g_ix, top_idx]
        key_pos = top_idx[..., None] * SLC_BLOCK + jnp.arange(SLC_BLOCK)
        m_s = (top_val >= 0.0)[..., None] & (key_pos <= t[None, None, :, None, None])
        s_s = jnp.einsum('bqgrd,bgqkld->bgrqkl', q_i, k_sel).astype(jnp.float32) * scale
        n_sel = top_k * SLC_BLOCK
        p_s = masked_softmax(s_s.reshape(B, G, R, Q_BLOCK, n_sel), m_s.reshape(B, G, 1, Q_BLOCK, n_sel))
        o_s = jnp.einsum('bgrqn,bgqnd->bqgrd', p_s.astype(v_sel.dtype),
                         v_sel.reshape(B, G, Q_BLOCK, n_sel, dh))
        k_wi = lax.dynamic_slice_in_dim(kw_pad, i * Q_BLOCK, span, axis=1)
        v_wi = lax.dynamic_slice_in_dim(vw_pad, i * Q_BLOCK, span, axis=1)
        pos_w = i * Q_BLOCK - WINDOW + jnp.arange(span)
        m_w = (pos_w[None, :] >= 0) & (pos_w[None, :] <= t[:, None]) & (pos_w[None, :] > t[:, None] - WINDOW)
        s_w = jnp.einsum('bqgrd,bkgd->bgrqk', q_i, k_wi).astype(jnp.float32) * scale
        p_w = masked_softmax(s_w, m_w)
        o_w = jnp.einsum('bgrqk,bkgd->bqgrd', p_w.astype(v_wi.dtype), v_wi)
        g = g_i.reshape(B, Q_BLOCK, 3, G, R)[..., None]
        return g[:, :, 0] * o_c + g[:, :, 1] * o_s + g[:, :, 2] * o_w

    out = lax.map(block, (qg, gb, jnp.arange(n_blk)))
    return out.transpose(1, 0, 2, 3, 4, 5).reshape(B, S, H * dh)


def spatial_gating(u, v, ln_g, w_s, b_s):
    B, S, W = u.shape
    n_chunk = S // SG_CHUNK
    u = jax.nn.gelu(u, approximate=False)
    v = layer_norm(jax.nn.gelu(v, approximate=False), ln_g)
    vc = v.reshape(B, n_chunk, SG_CHUNK, SG_GROUPS, SG_GROUP_DIM)
    causal = jnp.tril(jnp.ones((SG_CHUNK, SG_CHUNK), dtype=w_s.dtype))
    mixed = jnp.einsum('gts,bcsgd->bctgd', w_s * causal, vc) + b_s.T[None, None, :, :, None]
    return u * mixed.reshape(B, S, W)


def hybrid_layer(x, norm_g, w_in, b_forget, qn_a, kn_a, conv_w, conv_b, w_rg_a, b_rg_a,
                 w_rg_x, b_rg_x, lru_lambda, qn_c, kn_c, cmp_pos, cmp_k_w1, cmp_k_w2,
                 cmp_v_w1, cmp_v_w2, ln_v_g, w_spatial, b_spatial, w_branch, w_out):
    B, S, D = x.shape
    W = BRANCH_WIDTH
    xn = rms_norm(x, norm_g)
    z = xn @ w_in
    split_at = np.cumsum(_in_split_sizes())[:-1].tolist()
    (qa, ka, va, fa, ga, xb, gb, qc, kcc, vcc, ksc, vsc, kwc, vwc, gate_c, gc,
     ud, vd, gd, mg) = jnp.split(z, split_at, axis=-1)
    heads = lambda t, n: t.reshape(B, S, n, HEAD_DIM)
    y_a = fox_attention(rms_norm(heads(qa, FOX_HEADS), qn_a), rms_norm(heads(ka, FOX_HEADS), kn_a),
                        heads(va, FOX_HEADS), fa + b_forget).reshape(B, S, W)
    y_b = rg_lru(xb, conv_w, conv_b, w_rg_a, b_rg_a, w_rg_x, b_rg_x, lru_lambda)
    y_c = nsa_attention(rms_norm(heads(qc, NSA_HEADS), qn_c),
                        heads(kcc, NSA_KV_HEADS), heads(vcc, NSA_KV_HEADS),
                        heads(ksc, NSA_KV_HEADS), heads(vsc, NSA_KV_HEADS),
                        heads(kwc, NSA_KV_HEADS), heads(vwc, NSA_KV_HEADS),
                        jax.nn.sigmoid(gate_c).reshape(B, S, 3, NSA_HEADS),
                        kn_c, cmp_pos, cmp_k_w1, cmp_k_w2, cmp_v_w1, cmp_v_w2)
    y_d = spatial_gating(ud, vd, ln_v_g, w_spatial, b_spatial)
    ys = jnp.stack([y_a * jax.nn.silu(ga), y_b * jax.nn.silu(gb),
                    y_c * jax.nn.silu(gc), y_d * jax.nn.silu(gd)], axis=2)
    proj = jnp.einsum('bsnw,nwd->bsnd', ys, w_branch)
    merged = jnp.sum(jax.nn.sigmoid(mg.reshape(B, S, N_BRANCHES, D)) * proj, axis=2)
    return x + merged @ w_out


def setup_inputs(seed: int = 0) -> dict:
    key = jax.random.key(seed)
    ks = iter(jax.random.split(key, 32))
    L, D, W, dh = DEPTH, D_MODEL, BRANCH_WIDTH, HEAD_DIM

    def nrm(shape, scale):
        return jax.random.normal(next(ks), shape, jnp.float32) * scale

    w_in_width = sum(_in_split_sizes())
    x = nrm((BATCH, SEQ, D), 1.0)
    norm_g = 1.0 + nrm((L, D), 0.05)
    w_in = nrm((L, D, w_in_width), D ** -0.5)
    b_forget = 3.0 + nrm((L, FOX_HEADS), 0.5)
    qn_a = 1.0 + nrm((L, dh), 0.05)
    kn_a = 1.0 + nrm((L, dh), 0.05)
    conv_w = nrm((L, CONV_WIDTH, W), CONV_WIDTH ** -0.5)
    conv_b = nrm((L, W), 0.01)
    w_rg_a = nrm((L, LRU_BLOCKS, LRU_BLOCK_DIM, LRU_BLOCK_DIM), LRU_BLOCK_DIM ** -0.5)
    b_rg_a = nrm((L, W), 0.01)
    w_rg_x = nrm((L, LRU_BLOCKS, LRU_BLOCK_DIM, LRU_BLOCK_DIM), LRU_BLOCK_DIM ** -0.5)
    b_rg_x = nrm((L, W), 0.01)
    a_c = jax.random.uniform(next(ks), (L, W), jnp.float32, 0.9, 0.999)
    a0 = a_c ** (1.0 / LRU_C)
    lru_lambda = jnp.log(a0) - jnp.log1p(-a0)
    qn_c = 1.0 + nrm((L, dh), 0.05)
    kn_c = 1.0 + nrm((L, dh), 0.05)
    cmp_pos = nrm((L, CMP_BLOCK, dh), 0.02)
    cmp_k_w1 = nrm((L, CMP_BLOCK * dh, CMP_HIDDEN), (CMP_BLOCK * dh) ** -0.5)
    cmp_k_w2 = nrm((L, CMP_HIDDEN, dh), CMP_HIDDEN ** -0.5)
    cmp_v_w1 = nrm((L, CMP_BLOCK * dh, CMP_HIDDEN), (CMP_BLOCK * dh) ** -0.5)
    cmp_v_w2 = nrm((L, CMP_HIDDEN, dh), CMP_HIDDEN ** -0.5)
    ln_v_g = 1.0 + nrm((L, W), 0.05)
    w_spatial = nrm((L, SG_GROUPS, SG_CHUNK, SG_CHUNK), SG_CHUNK ** -0.5)
    b_spatial = 1.0 + nrm((L, SG_GROUPS, SG_CHUNK), 0.1)
    w_branch = nrm((L, N_BRANCHES, W, D), W ** -0.5)
    w_out = nrm((L, D, D), D ** -0.5)
    return {"x": x, "norm_g": norm_g, "w_in": w_in, "b_forget": b_forget, "qn_a": qn_a,
            "kn_a": kn_a, "conv_w": conv_w, "conv_b": conv_b, "w_rg_a": w_rg_a,
            "b_rg_a": b_rg_a, "w_rg_x": w_rg_x, "b_rg_x": b_rg_x, "lru_lambda": lru_lambda,
            "qn_c": qn_c, "kn_c": kn_c, "cmp_pos": cmp_pos, "cmp_k_w1": cmp_k_w1,
            "cmp_k_w2": cmp_k_w2, "cmp_v_w1": cmp_v_w1, "cmp_v_w2": cmp_v_w2,
            "ln_v_g": ln_v_g, "w_spatial": w_spatial, "b_spatial": b_spatial,
            "w_branch": w_branch, "w_out": w_out}


def reference(x, norm_g, w_in, b_forget, qn_a, kn_a, conv_w, conv_b, w_rg_a, b_rg_a,
              w_rg_x, b_rg_x, lru_lambda, qn_c, kn_c, cmp_pos, cmp_k_w1, cmp_k_w2,
              cmp_v_w1, cmp_v_w2, ln_v_g, w_spatial, b_spatial, w_branch, w_out):
    for l in range(DEPTH):
        x = hybrid_layer(x, norm_g[l], w_in[l], b_forget[l], qn_a[l], kn_a[l], conv_w[l],
                         conv_b[l], w_rg_a[l], b_rg_a[l], w_rg_x[l], b_rg_x[l], lru_lambda[l],
                         qn_c[l], kn_c[l], cmp_pos[l], cmp_k_w1[l], cmp_k_w2[l], cmp_v_w1[l],
                         cmp_v_w2[l], ln_v_g[l], w_spatial[l], b_spatial[l], w_branch[l], w_out[l])
    return x
```

```python
import math
from contextlib import ExitStack
import numpy as np
import concourse.bass as bass
import concourse.mybir as mybir
from concourse.bass_utils import run_bass_kernel_spmd

F32 = mybir.dt.float32
BF16 = mybir.dt.bfloat16
AF = mybir.ActivationFunctionType
ALU = mybir.AluOpType
AX = mybir.AxisListType

D = 1024
W = 512
WIN = 10528
BIG = 30000.0
EPS = 1e-6
O_QA, O_KA, O_VA, O_FA, O_GA = 0, 512, 1024, 1536, 1544
O_XB, O_GB = 2056, 2568
O_QC, O_KCC, O_VCC, O_KSC, O_VSC, O_KWC, O_VWC, O_GATE, O_GC = 3080, 3592, 3720, 3848, 3976, 4104, 4232, 4360, 4384
O_UD, O_VD, O_GD, O_MG = 4896, 5408, 5920, 6432
WB = 256
NH = 4
L_QA, L_KA, L_VA, L_FA, L_GA, L_XB, L_GB, L_QC = 0, 256, 512, 768, 772, 1028, 1284, 1540
L_KCC, L_VCC, L_KSC, L_KWC, L_VSC, L_VWC, L_GATE, L_GC = 1796, 1860, 1924, 1988, 2052, 2116, 2180, 2192
L_UD, L_VD, L_GD, L_MG, WM = 2448, 2704, 3216, 3472, 5520
RG = [[0, 1], [2, 3], [4, 5], [6, 7]]


def local_cols(h):
    r = lambda o, n: list(range(o, o + n))
    cols = []
    cols += r(O_QA + h * 256, 256) + r(O_KA + h * 256, 256) + r(O_VA + h * 256, 256) + r(O_FA + h * 4, 4)
    cols += r(O_GA + h * 256, 256) + r(O_XB + h * 256, 256) + r(O_GB + h * 256, 256) + r(O_QC + h * 256, 256)
    cols += r(O_KCC + h * 64, 64) + r(O_VCC + h * 64, 64) + r(O_KSC + h * 64, 64) + r(O_KWC + h * 64, 64)
    cols += r(O_VSC + h * 64, 64) + r(O_VWC + h * 64, 64)
    for br in range(3):
        cols += r(O_GATE + br * 8 + h * 4, 4)
    cols += r(O_GC + h * 256, 256) + r(O_UD + h * 256, 256)
    cols += r(O_VD + h * 256, 256) + r(O_VD + (1 - h) * 256, 256)
    cols += r(O_GD + h * 256, 256)
    for n in range(4):
        cols += r(O_MG + n * D + h * 512, 512)
    assert len(cols) == WM
    return np.asarray(cols)


class Prog:
    NDMA = 8

    def __init__(self, nc):
        self.nc = nc
        self.ops = []

    def add(self, q, fn, r=(), w=(), dma=False, barrier=False, cc=False):
        self.ops.append(dict(q=q, fn=fn, r=tuple(r), w=tuple(w), dma=dma or cc, barrier=barrier, cc=cc))

    def mm(self, out, lhsT, rhs, start, stop, r, w):
        self.add('pe', lambda e: e.matmul(out, lhsT, rhs, start=start, stop=stop), r, w)

    def tr(self, out, in_, ident, r, w):
        self.add('pe', lambda e: e.transpose(out, in_, ident), r, w)

    def act(self, out, in_, func, r, w, bias=None, scale=1.0, accum_out=None):
        def f(e):
            kw = {}
            if bias is not None:
                kw['bias'] = bias
            if accum_out is not None:
                kw['accum_out'] = accum_out
            return e.activation(out, in_, func, scale=scale, **kw)
        self.add('act', f, r, w)

    def v(self, q, name, r, w, *args, **kw):
        self.add(q, lambda e: getattr(e, name)(*args, **kw), r, w)

    def dma(self, q, out, in_, r, w, **kw):
        self.add(q, lambda e: e.dma_start(out, in_, **kw), r, w, dma=True)

    UK = [0]

    def uk(self, name):
        Prog.UK[0] += 1
        return (name, 'u', Prog.UK[0])

    def barrier(self):
        for q in ('pe', 'act', 'dve', 'pool', 'sp'):
            self.add(q, None, barrier=True)

    def finalize(self):
        nc = self.nc
        ops = self.ops
        last_w, rd_c, rd_d = {}, {}, {}
        last_q = {}
        recent_dma = {}
        for i, op in enumerate(ops):
            deps = set()
            if op['barrier']:
                for q, j in last_q.items():
                    deps.add(j)
                for q, lst in recent_dma.items():
                    deps.update(lst)
            for k in op['r']:
                if k in last_w:
                    deps.add(last_w[k])
            for k in op['w']:
                if k in last_w:
                    deps.add(last_w[k])
                deps.update(rd_c.get(k, {}).values())
                deps.update(rd_d.get(k, ()))
            deps.discard(i)
            op['deps'] = deps
            for k in op['w']:
                last_w[k] = i
                rd_c[k] = {}
                rd_d[k] = []
            for k in op['r']:
                if k in op['w']:
                    continue
                if op['dma']:
                    rd_d.setdefault(k, []).append(i)
                else:
                    rd_c.setdefault(k, {})[op['q']] = i
            if op['fn'] is not None:
                if op.get('cc'):
                    pass
                elif op['dma']:
                    lst = recent_dma.setdefault(op['q'], [])
                    lst.append(i)
                    if len(lst) > self.NDMA:
                        lst.pop(0)
                else:
                    last_q[op['q']] = i
        needed = [False] * len(ops)
        for op in ops:
            for j in op['deps']:
                needed[j] = True
        queues = sorted(set(op['q'] for op in ops))
        csem = {q: nc.alloc_semaphore('c_' + q) for q in queues}
        dsem = {q: [nc.alloc_semaphore('d_%s%d' % (q, k)) for k in range(self.NDMA)]
                for q in queues if any(o['dma'] and o['q'] == q for o in ops)}
        ccount = {q: 0 for q in queues}
        dcount = {q: 0 for q in queues}
        for i, op in enumerate(ops):
            q = op['q']
            op['sem'] = None
            op['pre'] = None
            if op['fn'] is None:
                continue
            if op.get('cc'):
                if 'cc' not in ccount:
                    ccount['cc'] = 0
                    self.ccsem = nc.alloc_semaphore('cc_sem')
                ccount['cc'] += 1
                op['sem'] = self.ccsem
                op['tick'] = ccount['cc']
                op['inc'] = None
                if ccount['cc'] > 1:
                    op['pre'] = (self.ccsem, ccount['cc'] - 1)
            elif op['dma']:
                n = dcount[q]
                dcount[q] += 1
                op['sem'] = dsem[q][n % self.NDMA]
                op['tick'] = 16 * (n // self.NDMA + 1)
                if n >= self.NDMA:
                    op['pre'] = (op['sem'], 16 * (n // self.NDMA))
                op['inc'] = 16
            elif needed[i]:
                ccount[q] += 1
                op['sem'] = csem[q]
                op['tick'] = ccount[q]
                op['inc'] = 1
        by_q = {q: [i for i, o in enumerate(ops) if o['q'] == q] for q in queues}
        self.n_waits = 0

        def run(q, eng):
            waited = {}
            for i in by_q[q]:
                op = ops[i]
                waits = []
                if op['pre'] is not None:
                    waits.append(op['pre'])
                for j in sorted(op['deps']):
                    dj = ops[j]
                    if dj['sem'] is None:
                        continue
                    if dj['q'] == q and q == 'pe' and not dj['dma']:
                        continue
                    waits.append((dj['sem'], dj['tick']))
                for sem, val in waits:
                    key = id(sem)
                    if waited.get(key, 0) >= val:
                        continue
                    waited[key] = val
                    eng.wait_ge(sem, val)
                    self.n_waits += 1
                if op['fn'] is None:
                    continue
                ins = op['fn'](eng)
                if op['sem'] is not None:
                    if op['inc'] is None:
                        ins.then_inc(op['sem'])
                    else:
                        ins.then_inc(op['sem'], op['inc'])

        emap = {'pe': 'tensor', 'act': 'scalar', 'dve': 'vector', 'pool': 'gpsimd', 'sp': 'sync'}
        with nc.Block() as block:
            for q in queues:
                getattr(block, emap[q])(lambda eng, q=q: run(q, eng))


class Rot:
    def __init__(self, alloc, name, shape, dtype, n):
        self.t = [alloc(name + str(i), shape, dtype) for i in range(n)]
        self.name = name
        self.i = 0

    def next(self):
        k = self.i % len(self.t)
        self.i += 1
        return self.t[k], (self.name, k)


class Scope:
    def __init__(self, nc):
        self.nc = nc
        self.st = ExitStack()

    UID = [0]

    def sb(self, name, shape, dt):
        Scope.UID[0] += 1
        return self.st.enter_context(self.nc.sbuf_tensor("%s_%d" % (name, Scope.UID[0]), shape, dt))

    def ps(self, name, shape, dt=F32):
        Scope.UID[0] += 1
        return self.st.enter_context(self.nc.psum_tensor("%s_%d" % (name, Scope.UID[0]), shape, dt))

    def close(self):
        self.st.close()


def host_consts(S):
    T = S // 128
    n_slc = S // 64
    p = np.arange(128)
    c = {}
    c['ident'] = np.eye(128, dtype=np.float32)
    c['bones'] = (p[:, None] // 64 == p[None, :] // 64).astype(np.float32)
    c['causal'] = np.where(p[:, None] > p[None, :], -BIG, 0.0).astype(np.float32)
    c['window'] = np.where(p[:, None] <= p[None, :], -BIG, 0.0).astype(np.float32)
    c['tril'] = (p[None, :] <= p[:, None]).astype(np.float32)
    E = np.zeros((128, T * 128), np.float32)
    for kj in range(T):
        for half in range(2):
            j = 2 * kj + half
            if j < 64:
                E[64 + j, kj * 128 + half * 64: kj * 128 + half * 64 + 64] = 1.0
    c['E'] = E
    u = np.arange(S)
    c['M0'] = np.where(u[None, :] >= 16 * p[:, None] + 31, 0.0, -BIG).astype(np.float32)
    n_cmp = (S - 32) // 16 + 1
    c0 = np.arange(256) * 16
    s0 = np.arange(64) * 64
    ov = np.minimum(c0[:, None] + 32, s0[None, :] + 64) - np.maximum(c0[:, None], s0[None, :])
    ov = (np.clip(ov, 0, None) / 32.0).astype(np.float32)
    ov[n_cmp:] = 0.0
    ov[:, n_slc:] = 0.0
    c['overlap'] = ov.reshape(2, 128, 64).transpose(1, 0, 2).reshape(128, 128)
    t = np.arange(S)
    jt = t // 64
    j = np.arange(64)
    valid = j[None, :] <= jt[:, None]
    forced = (j[None, :] == 0) | (valid & (j[None, :] > jt[:, None] - 2))
    cand = valid & ~forced
    tbl2 = np.where(forced, 1e6, np.where(cand, 0.0, -1.0)).astype(np.float32)
    if n_slc < 64:
        tbl2[:, n_slc:] = -1.0
    c['cand'] = cand.astype(np.float32).reshape(T, 128, 64).transpose(1, 0, 2).reshape(128, T * 64)
    c['tbl2'] = tbl2.reshape(T, 128, 64).transpose(1, 0, 2).reshape(128, T * 64)
    names = ['ident', 'bones', 'causal', 'window', 'tril', 'E', 'M0', 'overlap', 'cand', 'tbl2']
    offs = {}
    o = 0
    for n in names:
        offs[n] = (o, c[n].shape[1])
        o += c[n].shape[1]
    return np.concatenate([c[n] for n in names], axis=1), offs


PARAMS = [("norm_g", [D]), ("w_in", [D, WM]), ("b_forget", [NH]), ("qn_a", [64]), ("kn_a", [64]),
          ("conv_w", [4, WB]), ("conv_b", [WB]), ("w_rg_a", [4, 64, 64]), ("b_rg_a", [WB]),
          ("w_rg_x", [4, 64, 64]), ("b_rg_x", [WB]), ("lru_lambda", [WB]), ("qn_c", [64]),
          ("kn_c", [64]), ("cmp_pos", [32, 64]), ("cmp_k_w1", [2048, 128]), ("cmp_k_w2", [128, 64]),
          ("cmp_v_w1", [2048, 128]), ("cmp_v_w2", [128, 64]), ("ln_v_g", [W]),
          ("w_spatial", [4, 128, 128]), ("b_spatial", [4, 128]), ("w_branch", [4 * W, W]),
          ("w_out", [D, W])]


def slice_params(inputs, h):
    a = lambda k: np.asarray(inputs[k], dtype=np.float32)
    hs = slice(h * WB, (h + 1) * WB)
    p = {}
    p["norm_g"] = a("norm_g")
    p["w_in"] = a("w_in")[:, :, local_cols(h)]
    p["b_forget"] = a("b_forget")[:, h * NH:(h + 1) * NH]
    for k in ("qn_a", "kn_a", "qn_c", "kn_c", "cmp_pos", "cmp_k_w1", "cmp_k_w2", "cmp_v_w1", "cmp_v_w2"):
        p[k] = a(k)
    p["conv_w"] = a("conv_w")[:, :, hs]
    for k in ("conv_b", "b_rg_a", "b_rg_x", "lru_lambda"):
        p[k] = a(k)[:, hs]
    p["w_rg_a"] = a("w_rg_a")[:, h * 4:(h + 1) * 4]
    p["w_rg_x"] = a("w_rg_x")[:, h * 4:(h + 1) * 4]
    lg = a("ln_v_g")
    p["ln_v_g"] = np.concatenate([lg[:, hs], lg[:, (1 - h) * WB:(2 - h) * WB]], axis=1)
    p["w_spatial"] = a("w_spatial")[:, h * 4:(h + 1) * 4]
    p["b_spatial"] = a("b_spatial")[:, h * 4:(h + 1) * 4]
    wb = a("w_branch")
    Lr = wb.shape[0]
    p["w_branch"] = wb.reshape(Lr, 4 * W, D)[:, :, h * W:(h + 1) * W]
    wo = a("w_out")[:, :, h * W:(h + 1) * W]
    p["w_out"] = wo.reshape(Lr, 2, 2, WB, W).transpose(0, 2, 1, 3, 4).reshape(Lr, D, W)
    return {k: np.ascontiguousarray(v) for k, v in p.items()}


class Builder:
    def __init__(self, S, L, dbg=(), stop_after=None):
        self.S, self.L = S, L
        self.T = S // 128
        self.NC = S // 512
        self.dbg = set(dbg)
        self.stop_after = stop_after
        self.nc = bass.Bass("TRN2", target_bir_lowering=False)
        self.P = Prog(self.nc)
        self.consts_np, self.coffs = host_consts(S)

    def din(self, name, shape, dt=F32):
        return self.nc.dram_tensor(name, list(shape), dt, kind="ExternalInput").ap()

    def dscr(self, name, shape, dt):
        kind = "ExternalOutput" if name in self.dbg else "Internal"
        return self.nc.dram_tensor(name, list(shape), dt, kind=kind).ap()

    def build(self):
        nc, P, S, L, T = self.nc, self.P, self.S, self.L, self.T
        self.x = self.din("x", [S, D])
        self.xm = self.din("xm", [S, W])
        self.y = self.nc.dram_tensor("y", [S, W], F32, kind="ExternalOutput").ap()
        self.pr = {n: self.din(n, [L] + shp) for n, shp in PARAMS}
        self.cst = self.din("consts", list(self.consts_np.shape))
        d = self.dscr
        self.xnew = d("xnew", [S, W], F32)
        self.xres = d("xres", [2 * S, W], F32)
        self.w_in_b = d("w_in_b", [D, WM], BF16)
        self.w_br_b = d("w_br_b", [4 * W, W], BF16)
        self.w_out_b = d("w_out_b", [D, W], BF16)
        self.fox_qT = d("fox_qT", [NH, 70, S], BF16)
        self.fox_kT = d("fox_kT", [NH, 70, S], BF16)
        self.fox_v = d("fox_v", [S, NH, 65], BF16)
        self.ga_s = d("ga_s", [S, WB], F32)
        self.gc_s = d("gc_s", [S, WB], F32)
        self.y_a = d("y_a", [S, WB], F32)
        self.y_c = d("y_c", [S, WB], F32)
        self.nsa_qT = d("nsa_qT", [NH, 64, S], BF16)
        self.kcT = d("kcT", [64, S], BF16)
        self.vcT = d("vcT", [64, S], BF16)
        self.kslcT = d("kslcT", [64, S], BF16)
        self.kwinT = d("kwinT", [64, S], BF16)
        self.vslc = d("vslc", [128, (S // 128) * 65], BF16)
        self.vwin = d("vwin", [128, (S // 128) * 65], BF16)
        self.gates = d("gates", [128, (S // 128) * 12], F32)
        self.ysT = d("ysT", [4 * WB, S], BF16)
        self.ysT_all = d("ysT_all", [8 * WB, S], BF16)
        self.mT = d("mT", [W, S], BF16)
        self.mT_all = d("mT_all", [D, S], BF16)

        g = Scope(nc)
        self.g = g
        self.xnT = g.sb("xnT", [128, 8, S], BF16)
        self.c = {}
        gnames = ['ident', 'bones', 'causal', 'window', 'tril', 'overlap']
        self.alloc_consts(g, gnames)
        self.ones_f = g.sb("ones_f", [128, 1], F32)
        P.v('pool', 'memset', [], ['ones_f'], self.ones_f[:], 1.0)
        s0 = Scope(nc)
        self.stage = Rot(s0.sb, "stage", [128, 2048], F32, 2)
        self.load_consts(gnames)
        P.barrier()
        s0.close()
        for l in range(L):
            self.l = l
            xin = None
            xout = None
            if self.stop_after == 'C0':
                break
            self.phase_W()
            if self.stop_after == 'W':
                break
            self.phase_N(xin)
            if self.stop_after == 'N':
                break
            self.phase_J()
            if self.stop_after == 'J':
                break
            self.phase_A()
            if self.stop_after == 'A':
                break
            self.phase_C()
            if self.stop_after == 'C':
                break
            self.phase_G()
            if self.stop_after == 'G':
                break
            self.phase_M(xin, xout)
        P.barrier()
        P.finalize()
        g.close()
        return nc

    def alloc_consts(self, g, names):
        for n in names:
            o, w = self.coffs[n]
            self.c[n] = g.sb("c_" + n, [128, w], BF16)

    def load_consts(self, names):
        P = self.P
        for n in names:
            o, w = self.coffs[n]
            t = self.c[n]
            for c0 in range(0, w, 2048):
                cw = min(2048, w - c0)
                st, sk = self.stage.next()
                P.dma('sp', st[:, 0:cw], self.cst[:, o + c0:o + c0 + cw], [], [sk])
                P.v('dve', 'tensor_copy', [sk], ['c_' + n], t[:, c0:c0 + cw], st[:, 0:cw])

    def cast_dram(self, src, dst, R, C, dkey, cnt):
        P = self.P
        engs = ['dve', 'act']
        for r0 in range(0, R, 128):
            for c0 in range(0, C, 2048):
                cw = min(2048, C - c0)
                st, sk = self.stage.next()
                bt, bk = self.castb.next()
                P.dma('sp', st[:, 0:cw], src[r0:r0 + 128, c0:c0 + cw], [], [sk])
                e = engs[cnt[0] % 2]
                cnt[0] += 1
                if e == 'act':
                    P.act(bt[:, 0:cw], st[:, 0:cw], AF.Copy, [sk], [bk])
                else:
                    P.v(e, 'tensor_copy', [sk], [bk], bt[:, 0:cw], st[:, 0:cw])
                P.dma('pool', dst[r0:r0 + 128, c0:c0 + cw], bt[:, 0:cw], [bk], [P.uk(dkey)])

    def phase_W(self):
        P, l = self.P, self.l
        s = Scope(self.nc)
        self.stage = Rot(s.sb, "stage", [128, 2048], F32, 2)
        self.castb = Rot(s.sb, "castb", [128, 2048], BF16, 3)
        cnt = [0]
        self.cast_dram(self.pr['w_in'][l], self.w_in_b, D, WM, 'w_in_b', cnt)
        self.cast_dram(self.pr['w_branch'][l], self.w_br_b, 4 * W, W, 'w_br_b', cnt)
        self.cast_dram(self.pr['w_out'][l], self.w_out_b, D, W, 'w_out_b', cnt)
        P.barrier()
        s.close()

    def phase_N(self, xin):
        P, S, T, l = self.P, self.S, self.T, self.l
        s = Scope(self.nc)
        gb = s.sb("gb", [128, D], F32)
        P.dma('sp', gb[:], self.pr['norm_g'][l].partition_broadcast(128), [], ['gb'])
        xt = Rot(s.sb, "xt", [128, 4, D], F32, 2)
        junk = Rot(s.sb, "junk", [128, D], BF16, 2)
        ss = Rot(s.sb, "ss", [128, 4], F32, 2)
        rs = Rot(s.sb, "rsn", [128, 4], F32, 2)
        xn = Rot(s.sb, "xn", [128, D], BF16, 3)
        pst = Rot(s.ps, "pst", [128, 8, 128], BF16, 3)
        ident = self.c['ident']
        for g in range(T // 4):
            x_t, xk = xt.next()
            s_t, sk = ss.next()
            r_t, rk = rs.next()
            for q in range(4):
                t = g * 4 + q
                if self.l == 0:
                    P.dma('sp', x_t[:, q, :], self.x[t * 128:(t + 1) * 128, :], [], [(xk, q)])
                else:
                    for hf in range(2):
                        r0 = ((t // 8) * 2 + hf) * 1024 + (t % 8) * 128
                        P.dma('sp', x_t[:, q, hf * W:(hf + 1) * W], self.xres[r0:r0 + 128, :], ['xres'], [(xk, q, hf)])
            for q in range(4):
                j_t, jk = junk.next()
                rk_x = [(xk, q)] if self.l == 0 else [(xk, q, 0), (xk, q, 1)]
                P.act(j_t[:], x_t[:, q, :], AF.Square, rk_x, [jk, sk], accum_out=s_t[:, q:q + 1])
            P.act(r_t[:], s_t[:], AF.Ln, [sk], [rk], scale=1.0 / D, bias=EPS)
            P.act(r_t[:], r_t[:], AF.Exp, [rk], [rk], scale=-0.5)
            for q in range(4):
                t = g * 4 + q
                n_t, nk = xn.next()
                p_t, pk = pst.next()
                rk_x = [(xk, q)] if self.l == 0 else [(xk, q, 0), (xk, q, 1)]
                P.v('dve', 'scalar_tensor_tensor', rk_x + [rk, 'gb'], [nk], n_t[:], x_t[:, q, :], r_t[:, q:q + 1], gb[:], ALU.mult, ALU.mult)
                for kc in range(8):
                    P.tr(p_t[:, kc, :], n_t[:, kc * 128:(kc + 1) * 128], ident[:], [nk, 'c_ident'], [pk])
                if q % 2 == 0:
                    P.v('dve', 'tensor_copy', [pk], ['xnT'], self.xnT[:, :, t * 128:(t + 1) * 128], p_t[:])
                else:
                    P.act(self.xnT[:, :, t * 128:(t + 1) * 128], p_t[:], AF.Copy, [pk], ['xnT'])
        P.barrier()
        s.close()

    def load_w(self, c0, n):
        wt, wk = self.wrot.next()
        self.P.dma('sp', wt[:, :, 0:n], self.w_in_b[:, c0:c0 + n].rearrange("(kc p) n -> p kc n", p=128), ['w_in_b'], [wk])
        return wt, wk

    def fm(self, wt, wk, m0, m, tc, ps, pk):
        for kc in range(8):
            self.P.mm(ps[0:m, 0:512], wt[:, kc, m0:m0 + m], self.xnT[:, kc, tc * 512:(tc + 1) * 512], kc == 0, kc == 7, [wk, 'xnT'], [pk])

    def tm(self, wt, wk, n0, n, tt, ps, pk):
        for kc in range(8):
            self.P.mm(ps[:, 0:n], self.xnT[:, kc, tt * 128:(tt + 1) * 128], wt[:, kc, n0:n0 + n], kc == 0, kc == 7, [wk, 'xnT'], [pk])

    def load_col(self, s, name, src, scale=None):
        t = s.sb(name, [128, 1], F32)
        for h in range(2):
            self.P.dma('sp', t[h * 64:(h + 1) * 64, :], src.rearrange("(p o) -> p o", o=1), [], [name])
        if scale is not None:
            self.P.v('dve', 'tensor_scalar', [name], [name], t[:], t[:], float(scale), None, ALU.mult)
        return t

    def norm_epilogue2(self, z, zk, sq, sqk, gcol, gk, out, okeys):
        P = self.P
        ps2, p2k = self.ps2rot.next()
        rs, rsk = self.rsrot.next()
        P.mm(ps2[:, 0:512], self.c['bones'][:], sq[:], True, True, [sqk, 'c_bones'], [p2k])
        P.act(rs[:], ps2[:, 0:512], AF.Ln, [p2k], [rsk], scale=1.0 / 64, bias=EPS)
        P.act(rs[:], rs[:], AF.Exp, [rsk], [rsk], scale=-0.5)
        P.v('dve', 'scalar_tensor_tensor', [zk, gk, rsk], okeys, out, z[:, 0:512], gcol[:, 0:1], rs[:], ALU.mult, ALU.mult)

    def norm_epilogue(self, z, zk, gcol, gk, out, okeys):
        P = self.P
        sq, sqk = self.sqrot.next()
        ps2, p2k = self.ps2rot.next()
        rs, rsk = self.rsrot.next()
        P.act(sq[:], z[:, 0:512], AF.Square, [zk], [sqk])
        P.mm(ps2[:, 0:512], self.c['bones'][:], sq[:], True, True, [sqk, 'c_bones'], [p2k])
        P.act(rs[:], ps2[:, 0:512], AF.Ln, [p2k], [rsk], scale=1.0 / 64, bias=EPS)
        P.act(rs[:], rs[:], AF.Exp, [rsk], [rsk], scale=-0.5)
        P.v('dve', 'scalar_tensor_tensor', [zk, gk, rsk], okeys, out, z[:, 0:512], gcol[:, 0:1], rs[:], ALU.mult, ALU.mult)

    def phase_J(self):
        P = self.P
        s = Scope(self.nc)
        self.wrot = Rot(s.sb, "wt", [128, 8, 512], BF16, 3)
        psz = Rot(s.ps, "psz", [128, 512], F32, 4)
        import os
        stj = int(os.environ.get("STOPJ", "9"))
        if stj >= 1:
            self.phase_J_norm(psz)
        if stj >= 2:
            self.phase_J_forget(psz)
        if stj >= 3:
            self.phase_J_tok(psz)
        if stj >= 4:
            self.phase_J_B(psz)
        if stj >= 5:
            self.phase_J_D(psz)
        P.barrier()
        s.close()

    def phase_J_norm(self, psz):
        P, S, T, NC, l = self.P, self.S, self.T, self.NC, self.l
        pr = self.pr
        s = Scope(self.nc)
        self.sqrot = Rot(s.sb, "sq", [128, 512], BF16, 3)
        self.rsrot = Rot(s.sb, "rs", [128, 512], F32, 2)
        self.ps2rot = Rot(s.ps, "ps2", [128, 512], F32, 2)
        outT = Rot(s.sb, "outT", [128, S], BF16, 2)
        g_qa = self.load_col(s, "g_qa", pr['qn_a'][l], 0.125)
        g_ka = self.load_col(s, "g_ka", pr['kn_a'][l])
        g_qc = self.load_col(s, "g_qc", pr['qn_c'][l], 0.125)
        g_kc = self.load_col(s, "g_kc", pr['kn_c'][l])

        def normed_block(c0, gcol, gk, dsts):
            wt, wk = self.load_w(c0, 128)
            o_t, ok = outT.next()
            pend = None
            for tc in range(NC):
                z, zk = psz.next()
                self.fm(wt, wk, 0, 128, tc, z, zk)
                sq, sqk = self.sqrot.next()
                P.act(sq[:], z[:, 0:512], AF.Square, [zk], [sqk])
                if pend is not None:
                    self.norm_epilogue2(*pend)
                pend = (z, zk, sq, sqk, gcol, gk, o_t[:, tc * 512:(tc + 1) * 512], [ok])
            self.norm_epilogue2(*pend)
            for h, (dst, dk) in enumerate(dsts):
                P.dma('pool', dst, o_t[h * 64:(h + 1) * 64, :], [ok], [dk])

        def raw_block(c0, dsts):
            wt, wk = self.load_w(c0, 128)
            o_t, ok = outT.next()
            for tc in range(NC):
                z, zk = psz.next()
                self.fm(wt, wk, 0, 128, tc, z, zk)
                P.act(o_t[:, tc * 512:(tc + 1) * 512], z[:, 0:512], AF.Copy, [zk], [ok])
            for h, (dst, dk) in enumerate(dsts):
                P.dma('pool', dst, o_t[h * 64:(h + 1) * 64, :], [ok], [dk])

        for hb in range(2):
            normed_block(L_QA + hb * 128, g_qa, "g_qa",
                         [(self.fox_qT[2 * hb, 0:64, :], 'fox_qT'), (self.fox_qT[2 * hb + 1, 0:64, :], 'fox_qT')])
            normed_block(L_KA + hb * 128, g_ka, "g_ka",
                         [(self.fox_kT[2 * hb, 0:64, :], 'fox_kT'), (self.fox_kT[2 * hb + 1, 0:64, :], 'fox_kT')])
        for hb in range(2):
            normed_block(L_QC + hb * 128, g_qc, "g_qc",
                         [(self.nsa_qT[2 * hb, :, :], 'nsa_qT'), (self.nsa_qT[2 * hb + 1, :, :], 'nsa_qT')])
        raw_block(L_KCC, [(self.kcT, 'kcT'), (self.vcT, 'vcT')])
        normed_block(L_KSC, g_kc, "g_kc", [(self.kslcT, 'kslcT'), (self.kwinT, 'kwinT')])
        P.barrier()
        s.close()

    def phase_J_forget(self, psz):
        P, S, T, NC, l = self.P, self.S, self.T, self.NC, self.l
        pr = self.pr
        s2 = Scope(self.nc)
        negb = s2.sb("negb", [NH, 1], F32)
        P.dma('sp', negb[:], pr['b_forget'][l].rearrange("(p o) -> p o", o=1), [], ['negb'])
        P.v('dve', 'tensor_scalar', ['negb'], ['negb'], negb[:], negb[:], -1.0, None, ALU.mult)
        ones3 = s2.sb("ones3", [NH, 3, 512], BF16)
        P.v('pool', 'memset', [], ['ones3'], ones3[:], 1.0)
        wt, wk = self.load_w(L_FA, NH)
        fe = Rot(s2.sb, "fe", [NH, 512], F32, 2)
        fcs = Rot(s2.sb, "fcs", [NH, 512], F32, 2)
        fr = Rot(s2.sb, "fr", [NH, 512], F32, 2)
        fparts = Rot(s2.sb, "fparts", [NH, 3, 512], BF16, 2)
        fneg = Rot(s2.sb, "fneg", [NH, 3, 512], BF16, 2)
        prev = None
        for tc in range(NC):
            z, zk = psz.next()
            self.fm(wt, wk, 0, NH, tc, z, zk)
            e_t, ek = fe.next()
            c_t, ck = fcs.next()
            r_t, rk = fr.next()
            p_t, pk = fparts.next()
            n_t, nk = fneg.next()
            P.act(e_t[:], z[0:NH, 0:512], AF.Exp, [zk, 'negb'], [ek], scale=-1.0, bias=negb[:, 0:1])
            P.act(e_t[:], e_t[:], AF.Ln, [ek], [ek], bias=1.0)
            init = 0.0 if prev is None else prev[0][:, 511:512]
            rkeys = [ek, 'ones_f'] + ([] if prev is None else [prev[1]])
            P.v('dve', 'tensor_tensor_scan', rkeys, [ck], c_t[:], self.ones_f[0:NH, 0:1].to_broadcast([NH, 512]), e_t[:], init, ALU.mult, ALU.add)
            prev = (c_t, ck)
            P.v('dve', 'tensor_copy', [ck], [pk], p_t[:, 0, :], c_t[:])
            P.v('dve', 'tensor_tensor', [ck, pk], [rk], r_t[:], c_t[:], p_t[:, 0, :], ALU.subtract)
            P.v('dve', 'tensor_copy', [rk], [pk], p_t[:, 1, :], r_t[:])
            P.v('dve', 'tensor_tensor', [rk, pk], [rk], r_t[:], r_t[:], p_t[:, 1, :], ALU.subtract)
            P.v('dve', 'tensor_copy', [rk], [pk], p_t[:, 2, :], r_t[:])
            P.v('dve', 'tensor_scalar', [pk], [nk], n_t[:], p_t[:], -1.0, None, ALU.mult)
            sl = slice(tc * 512, (tc + 1) * 512)
            P.dma('pool', self.fox_kT[:, 64:67, sl], p_t[:], [pk], ['fox_kT'])
            P.dma('pool', self.fox_kT[:, 67:70, sl], ones3[:], ['ones3'], ['fox_kT'])
            P.dma('pool', self.fox_qT[:, 64:67, sl], ones3[:], ['ones3'], ['fox_qT'])
            P.dma('pool', self.fox_qT[:, 67:70, sl], n_t[:], [nk], ['fox_qT'])
        P.barrier()
        s2.close()

    def phase_J_tok(self, psz):
        P, S, T, NC, l = self.P, self.S, self.T, self.NC, self.l
        s2 = Scope(self.nc)
        vaug = Rot(s2.sb, "vaug", [128, NH, 65], BF16, 6)
        for i_, t_ in enumerate(vaug.t):
            P.v('pool', 'memset', [], [('vaug', i_)], t_[:], 1.0)
        wt, wk = self.load_w(L_VA, WB)
        for tt in range(T):
            z, zk = psz.next()
            self.tm(wt, wk, 0, WB, tt, z, zk)
            v_t, vk = vaug.next()
            P.v('dve', 'tensor_copy', [zk], [vk], v_t[:, :, 0:64], z[:, 0:WB].rearrange("p (h d) -> p h d", d=64))
            P.dma('pool', self.fox_v[tt * 128:(tt + 1) * 128, :, :], v_t[:], [vk], [P.uk('fox_v')])
        gsb = Rot(s2.sb, "gsb", [128, WB], F32, 8)
        for (c0, dst, dk) in ((L_GA, self.ga_s, 'ga_s'), (L_GC, self.gc_s, 'gc_s')):
            wt, wk = self.load_w(c0, WB)
            for tt in range(T):
                z, zk = psz.next()
                self.tm(wt, wk, 0, WB, tt, z, zk)
                g_t, gk = gsb.next()
                P.act(g_t[:], z[:, 0:WB], AF.Silu, [zk], [gk])
                P.dma('pool', dst[tt * 128:(tt + 1) * 128, :], g_t[:], [gk], [P.uk(dk)])
        vsl = s2.sb("vsl_all", [128, T, 65], BF16)
        vwn = s2.sb("vwn_all", [128, T, 65], BF16)
        P.v('pool', 'memset', [], ['vsl_all'], vsl[:], 1.0)
        P.v('pool', 'memset', [], ['vwn_all'], vwn[:], 1.0)
        gat = s2.sb("gat", [128, T, 12], F32)
        wt, wk = self.load_w(L_VSC, 140)
        for tt in range(T):
            z, zk = psz.next()
            self.tm(wt, wk, 0, 140, tt, z, zk)
            P.v('dve', 'tensor_copy', [zk], ['vsl_all'], vsl[:, tt, 0:64], z[:, 0:64])
            P.v('dve', 'tensor_copy', [zk], ['vwn_all'], vwn[:, tt, 0:64], z[:, 64:128])
            P.act(gat[:, tt, :], z[:, 128:140], AF.Sigmoid, [zk], ['gat'])
        P.dma('pool', self.vslc.rearrange("p (t c) -> p t c", c=65), vsl[:], ['vsl_all'], ['vslc'])
        P.dma('pool', self.vwin.rearrange("p (t c) -> p t c", c=65), vwn[:], ['vwn_all'], ['vwin'])
        P.dma('pool', self.gates.rearrange("p (t c) -> p t c", c=12), gat[:], ['gat'], ['gates'])
        P.barrier()
        s2.close()

    def phase_J_B(self, psz):
        P, S, NC, l = self.P, self.S, self.NC, self.l
        pr = self.pr
        s = Scope(self.nc)
        col = lambda v: v.rearrange("(p o) -> p o", o=1)
        prm = s.sb("b_prm", [128, 2, 8], F32)
        for cb in range(2):
            sl = slice(cb * 128, (cb + 1) * 128)
            for j in range(4):
                P.dma('sp', prm[:, cb, j:j + 1], col(pr['conv_w'][l, j, sl]), [], ['b_prm'])
            for j, nm in enumerate(('conv_b', 'b_rg_a', 'b_rg_x', 'lru_lambda')):
                P.dma('sp', prm[:, cb, 4 + j:5 + j], col(pr[nm][l, sl]), [], ['b_prm'])
        nsp8 = s.sb("nsp8", [128, 2, 1], F32)
        P.act(nsp8[:], prm[:, :, 7:8], AF.Exp, ['b_prm'], ['nsp8'], scale=-1.0)
        P.act(nsp8[:], nsp8[:], AF.Ln, ['nsp8'], ['nsp8'], bias=1.0)
        P.v('dve', 'tensor_scalar', ['nsp8'], ['nsp8'], nsp8[:], nsp8[:], -8.0, None, ALU.mult)
        wgf = s.sb("wgf", [128, 2, 2, 128], F32)
        P.v('pool', 'memset', [], ['wgf'], wgf[:], 0.0)
        for wi, nm in enumerate(('w_rg_a', 'w_rg_x')):
            for cb in range(2):
                for h in range(2):
                    P.dma('sp', wgf[h * 64:(h + 1) * 64, wi, cb, h * 64:(h + 1) * 64], pr[nm][l, 2 * cb + h], [], ['wgf'])
        wg = s.sb("wg", [128, 2, 2, 128], BF16)
        P.v('dve', 'tensor_copy', ['wgf'], ['wg'], wg[:], wgf[:])
        xbe = Rot(s.sb, "xbe", [128, 515], F32, 4)
        xc = Rot(s.sb, "xc", [128, 512], F32,3)
        xcb = Rot(s.sb, "xcb", [128, 512], BF16,2)
        rr = Rot(s.sb, "rr", [128, 512], F32,2)
        ig = Rot(s.sb, "ig", [128, 512], F32,3)
        aa = Rot(s.sb, "aa", [128, 512], F32,2)
        a2 = Rot(s.sb, "a2", [128, 512], F32,2)
        hh = Rot(s.sb, "hh", [128, 512], F32, 4)
        sg = Rot(s.sb, "sg", [128, 512], F32,2)
        wx = [s.sb("b_wx%d" % cb, [128, 8, 128], BF16) for cb in range(2)]
        wgt = [s.sb("b_wg%d" % cb, [128, 8, 128], BF16) for cb in range(2)]
        ysB = [s.sb("ysB%d" % cb, [128, S], BF16) for cb in range(2)]
        for cb in range(2):
            P.dma('sp', wx[cb][:], self.w_in_b[:, L_XB + cb * 128:L_XB + (cb + 1) * 128].rearrange("(kc p) n -> p kc n", p=128), ['w_in_b'], ['b_wx%d' % cb])
            P.dma('sp', wgt[cb][:], self.w_in_b[:, L_GB + cb * 128:L_GB + (cb + 1) * 128].rearrange("(kc p) n -> p kc n", p=128), ['w_in_b'], ['b_wg%d' % cb])
        prev_h = [None, None]
        prev_x = [None, None]
        for tc in range(NC):
            st = [dict(), dict()]
            for cb in range(2):
                d = st[cb]
                x_t, xk = xbe.next()
                z, zk = psz.next()
                self.fm(wx[cb], 'b_wx%d' % cb, 0, 128, tc, z, zk)
                P.act(x_t[:, 3:515], z[:, 0:512], AF.Copy, [zk], [xk])
                if prev_x[cb] is None:
                    P.v('pool', 'memset', [], [xk], x_t[:, 0:3], 0.0)
                else:
                    P.v('pool', 'tensor_copy', [prev_x[cb][1]], [xk], x_t[:, 0:3], prev_x[cb][0][:, 512:515])
                prev_x[cb] = (x_t, xk)
                c_t, ck = xc.next()
                P.v('dve', 'tensor_scalar', [xk, 'b_prm'], [ck], c_t[:], x_t[:, 3:515], prm[:, cb, 3:4], prm[:, cb, 4:5], ALU.mult, ALU.add)
                for j in range(3):
                    P.v('dve', 'scalar_tensor_tensor', [xk, 'b_prm', ck], [ck], c_t[:], x_t[:, j:j + 512], prm[:, cb, j:j + 1], c_t[:], ALU.mult, ALU.add)
                cb_t, cbk = xcb.next()
                P.v('dve', 'tensor_copy', [ck], [cbk], cb_t[:], c_t[:])
                d.update(c=(c_t, ck), cbt=(cb_t, cbk))
            for cb in range(2):
                d = st[cb]
                cb_t, cbk = d['cbt']
                za, zak = psz.next()
                P.mm(za[:, 0:512], wg[:, 0, cb, :], cb_t[:], True, True, ['wg', cbk], [zak])
                r_t, rk = rr.next()
                P.act(r_t[:], za[:, 0:512], AF.Sigmoid, [zak, 'b_prm'], [rk], bias=prm[:, cb, 5:6])
                zx, zxk = psz.next()
                P.mm(zx[:, 0:512], wg[:, 1, cb, :], cb_t[:], True, True, ['wg', cbk], [zxk])
                i_t, ik = ig.next()
                P.act(i_t[:], zx[:, 0:512], AF.Sigmoid, [zxk, 'b_prm'], [ik], bias=prm[:, cb, 6:7])
                d.update(r=(r_t, rk), i=(i_t, ik))
            for cb in range(2):
                d = st[cb]
                r_t, rk = d['r']
                a_t, ak = aa.next()
                P.act(a_t[:], r_t[:], AF.Exp, [rk, 'nsp8'], [ak], scale=nsp8[:, cb, 0:1])
                q_t, qk = a2.next()
                P.v('pool', 'tensor_tensor', [ak], [qk], q_t[:], a_t[:], a_t[:], ALU.mult)
                d.update(a=(a_t, ak), q=(q_t, qk))
            for cb in range(2):
                q_t, qk = st[cb]['q']
                P.act(q_t[:], q_t[:], AF.Sqrt, [qk], [qk], scale=-1.0, bias=1.0)
            for cb in range(2):
                zg, zgk = psz.next()
                self.fm(wgt[cb], 'b_wg%d' % cb, 0, 128, tc, zg, zgk)
                s_t, sk = sg.next()
                P.act(s_t[:], zg[:, 0:512], AF.Silu, [zgk], [sk])
                st[cb]['s'] = (s_t, sk)
            for cb in range(2):
                d = st[cb]
                i_t, ik = d['i']
                c_t, ck = d['c']
                q_t, qk = d['q']
                a_t, ak = d['a']
                s_t, sk = d['s']
                P.v('dve', 'tensor_tensor', [ik, ck], [ik], i_t[:], i_t[:], c_t[:], ALU.mult)
                P.v('dve', 'tensor_tensor', [ik, qk], [ik], i_t[:], i_t[:], q_t[:], ALU.mult)
                h_t, hk = hh.next()
                init = 0.0 if prev_h[cb] is None else prev_h[cb][0][:, 511:512]
                rkeys = [ak, ik] + ([] if prev_h[cb] is None else [prev_h[cb][1]])
                P.v('dve', 'tensor_tensor_scan', rkeys, [hk], h_t[:], a_t[:], i_t[:], init, ALU.mult, ALU.add)
                prev_h[cb] = (h_t, hk)
                P.v('dve', 'tensor_tensor', [hk, sk], ['ysB%d' % cb], ysB[cb][:, tc * 512:(tc + 1) * 512], h_t[:], s_t[:], ALU.mult)
        for cb in range(2):
            P.dma('pool', self.ysT[WB + cb * 128:WB + (cb + 1) * 128, :], ysB[cb][:], ['ysB%d' % cb], [('ysT', 1)])
        self.allgather(self.ysT[WB:2 * WB, :], self.ysT_all[W:2 * W, :], [('ysT', 1)], ['ysT_all'])
        P.barrier()
        s.close()

    def phase_J_D(self, psz):
        P, S, NC, l = self.P, self.S, self.NC, self.l
        pr = self.pr
        s = Scope(self.nc)
        wsf = s.sb("wsf", [128, 4, 128], F32)
        P.dma('sp', wsf[:], pr['w_spatial'][l].rearrange("g t s -> t g s"), [], ['wsf'])
        wsb = s.sb("wsb", [128, 4, 128], BF16)
        P.v('dve', 'tensor_tensor', ['wsf', 'c_tril'], ['wsb'], wsb[:], wsf[:], self.c['tril'][:].unsqueeze(1).to_broadcast([128, 4, 128]), ALU.mult)
        wsT = s.sb("wsT", [128, 4, 128], BF16)
        pst = s.ps("psT_d", [128, 4, 128], BF16)
        for g in range(4):
            P.tr(pst[:, g, :], wsb[:, g, :], self.c['ident'][:], ['wsb', 'c_ident'], ['psT_d'])
        P.v('dve', 'tensor_copy', ['psT_d'], ['wsT'], wsT[:], pst[:])
        bsb = s.sb("bsb", [128, 2, 128], F32)
        for gp in range(2):
            for h in range(2):
                P.dma('sp', bsb[h * 64:(h + 1) * 64, gp, :], pr['b_spatial'][l, 2 * gp + h].partition_broadcast(64), [], ['bsb'])
        lng = s.sb("lng", [128, W], F32)
        P.dma('sp', lng[:], pr['ln_v_g'][l].partition_broadcast(128), [], ['lng'])
        wtu, wku = self.load_w(L_UD, WB)
        wtv, wkv = self.load_w(L_VD, 512)
        wtg, wkg = self.load_w(L_GD, WB)
        ug = Rot(s.sb, "ug", [128, 2, 512], F32, 2)
        usb = Rot(s.sb, "usb", [128, 512], F32, 2)
        gsb = Rot(s.sb, "gsb_d", [128, 512], F32, 2)
        junk = s.sb("junk_d", [128, 512], BF16)
        vln = Rot(s.sb, "vln", [128, 512], BF16, 2)
        psm = Rot(s.ps, "psm", [128, 2, 128], F32, 2)
        t1 = Rot(s.sb, "t1_d", [128, 2, 128], F32, 2)
        ysd = Rot(s.sb, "ysd", [128, 2, 512], BF16, 2)
        gv = Rot(s.sb, "gv4", [128, 512], F32, 5)
        st4 = Rot(s.sb, "st4", [128, 4, 4], F32, 2)
        for tc in range(NC):
            u_t, uk = ug.next()
            bts = []
            for blk in range(2):
                z2, z2k = psz.next()
                self.fm(wtg, wkg, blk * 128, 128, tc, z2, z2k)
                b_t, bk = gsb.next()
                P.act(b_t[:], z2[:, 0:512], AF.Silu, [z2k], [bk])
                bts.append((b_t, bk))
            for blk in range(2):
                z, zk = psz.next()
                self.fm(wtu, wku, blk * 128, 128, tc, z, zk)
                a_t, ak = usb.next()
                P.act(a_t[:], z[:, 0:512], AF.Gelu, [zk], [ak])
                b_t, bk = bts[blk]
                P.v('pool', 'tensor_tensor', [ak, bk], [uk], u_t[:, blk, :], a_t[:], b_t[:], ALU.mult)
            s_t, sk = st4.next()
            gts_ = []
            for q in range(4):
                tt = tc * 4 + q
                z, zk = psz.next()
                self.tm(wtv, wkv, 0, 512, tt, z, zk)
                g_t, gk = gv.next()
                P.act(g_t[:], z[:, 0:512], AF.Gelu, [zk], [gk, sk], accum_out=s_t[:, q, 0:1])
                gts_.append((g_t, gk))
            P.v('dve', 'tensor_scalar', [sk], [sk], s_t[:, :, 1:2], s_t[:, :, 0:1], -1.0 / W, None, ALU.mult)
            for q in range(4):
                g_t, gk = gts_[q]
                P.act(junk[:], g_t[:], AF.Square, [gk, sk], ['junk_d', sk], bias=s_t[:, q, 1:2], accum_out=s_t[:, q, 2:3])
            P.act(s_t[:, :, 3:4], s_t[:, :, 2:3], AF.Ln, [sk], [sk], scale=1.0 / W, bias=EPS)
            P.act(s_t[:, :, 3:4], s_t[:, :, 3:4], AF.Exp, [sk], [sk], scale=-0.5)
            y_t, yk = ysd.next()
            for q in range(4):
                g_t, gk = gts_[q]
                P.v('dve', 'tensor_scalar', [gk, sk], [gk], g_t[:], g_t[:], s_t[:, q, 1:2], s_t[:, q, 3:4], ALU.add, ALU.mult)
                v_t, vk = vln.next()
                P.v('dve', 'tensor_tensor', [gk, 'lng'], [vk], v_t[:], g_t[:], lng[:], ALU.mult)
                m_t, mk = psm.next()
                for g in range(4):
                    P.mm(m_t[(g % 2) * 64:(g % 2) * 64 + 64, g // 2, :], v_t[:, g * 64:(g + 1) * 64], wsT[:, g, :], True, True, [vk, 'wsT'], [mk])
                a_t, ak = t1.next()
                P.v('dve', 'tensor_tensor', [mk, 'bsb'], [ak], a_t[:], m_t[:], bsb[:], ALU.add)
                P.v('dve', 'tensor_tensor', [ak, uk], [yk], y_t[:, :, q * 128:(q + 1) * 128], a_t[:], u_t[:, :, q * 128:(q + 1) * 128], ALU.mult)
            P.dma('pool', self.ysT[3 * WB:4 * WB, tc * 512:(tc + 1) * 512].rearrange("(b p) n -> p b n", p=128), y_t[:], [yk], [('ysT', 3)])
        self.allgather(self.ysT[3 * WB:4 * WB, :], self.ysT_all[3 * W:4 * W, :], [('ysT', 3)], ['ysT_all'])
        P.barrier()
        s.close()

    def attn_chunk(self, qT, qk, ktiles, acc, acck, ncols, pss, pT):
        P = self.P
        n = len(ktiles)
        st = [None] * n
        LOOK = 2

        def emit_S(i):
            kt = ktiles[i]
            s_t, sk = pss.next()
            q0 = kt['q0']
            q1 = kt.get('q1', 512)
            masks = kt['masks']
            P.mm(s_t[:, q0:q1], kt['lhsT'], qT[:, q0:q1], True, len(masks) == 0, kt['keys'] + [qk], [sk])
            for mi, (ml, mr, c0, c1, mkeys) in enumerate(masks):
                P.mm(s_t[:, c0:c1], ml, mr, False, mi == len(masks) - 1, mkeys, [sk])
            st[i] = (s_t, sk)

        for i in range(min(LOOK, n)):
            emit_S(i)
        for i in range(n):
            if i + LOOK < n:
                emit_S(i + LOOK)
            kt = ktiles[i]
            s_t, sk = st[i]
            q0 = kt['q0']
            q1 = kt.get('q1', 512)
            p_t, pk = pT.next()
            P.act(p_t[:, q0:q1], s_t[:, q0:q1], AF.Exp, [sk], [pk])
            for qq in kt['qts']:
                P.mm(acc[:, qq, 0:ncols], p_t[:, qq * 128:(qq + 1) * 128], kt['v'], kt['first'][qq], kt['last'][qq], [pk] + kt['vkeys'], [acck])

    def phase_A(self):
        P, S, T, NC = self.P, self.S, self.T, self.NC
        s = Scope(self.nc)
        kT = Rot(s.sb, "a_kT", [70, S], BF16, 2)
        qT = Rot(s.sb, "a_qT", [70, S], BF16, 2)
        va = Rot(s.sb, "a_v", [128, T, 65], BF16, 2)
        pss = Rot(s.ps, "a_ps", [128, 512], F32, 3)
        accs = Rot(s.ps, "a_acc", [128, 4, 512], F32, 1)
        pT = Rot(s.sb, "a_pT", [128, 512], BF16, 3)
        rden = Rot(s.sb, "a_rd", [128, 4, 1], F32, 2)
        osb = Rot(s.sb, "a_o", [128, 4, 64], F32, 2)
        ident, causal = self.c['ident'], self.c['causal']
        for h in range(NH):
            k_t, kk = kT.next()
            q_t, qk = qT.next()
            v_t, vk = va.next()
            P.dma('sp', k_t[:], self.fox_kT[h], ['fox_kT'], [kk])
            P.dma('sp', q_t[:], self.fox_qT[h], ['fox_qT'], [qk])
            P.dma('sp', v_t[:], self.fox_v[:, h, :].rearrange("(t p) c -> p t c", p=128), ['fox_v'], [vk])
            for tc in range(NC):
                acc, acck = accs.next()
                ktiles = []
                for kj in range(4 * tc + 4):
                    diag = kj >= 4 * tc
                    qlo = max(kj, 4 * tc) - 4 * tc
                    masks = []
                    if diag:
                        masks.append((ident[:], causal[:], qlo * 128, qlo * 128 + 128, ['c_ident', 'c_causal']))
                    qts = list(range(qlo, 4))
                    ktiles.append(dict(lhsT=k_t[:, kj * 128:(kj + 1) * 128], keys=[kk], v=v_t[:, kj, :], vkeys=[vk],
                                       q0=qlo * 128, masks=masks, qts=qts,
                                       first={qq: kj == 0 for qq in qts}, last={qq: kj == 4 * tc + qq for qq in qts}))
                self.attn_chunk(q_t[:, tc * 512:(tc + 1) * 512], qk, ktiles, acc, acck, 65, pss, pT)
                r_t, rk = rden.next()
                o_t, ok = osb.next()
                P.v('dve', 'reciprocal', [acck], [rk], r_t[:], acc[:, :, 64:65])
                P.v('dve', 'tensor_tensor', [acck, rk], [ok], o_t[:], acc[:, :, 0:64], r_t[:].to_broadcast([128, 4, 64]), ALU.mult)
                P.dma('pool', self.y_a[tc * 512:(tc + 1) * 512, h * 64:(h + 1) * 64].rearrange("(q p) d -> p q d", p=128), o_t[:], [ok], [P.uk('y_a')])
        P.barrier()
        s.close()

    def phase_C(self):
        P, S, T, NC, l = self.P, self.S, self.T, self.NC, self.l
        pr = self.pr
        c = self.c
        n_cmp = (S - 32) // 16 + 1
        NCT = (n_cmp + 127) // 128
        s = Scope(self.nc)
        self.alloc_consts(s, ['E', 'M0', 'cand', 'tbl2'])
        self.stage = Rot(s.sb, "stage_c", [128, 2048], F32, 2)
        self.load_consts(['E', 'M0', 'cand', 'tbl2'])
        w1 = {}
        w2 = {}
        for nm in ('k', 'v'):
            w1[nm] = s.sb("w1_" + nm, [64, 32, 128], BF16)
            src = pr['cmp_%s_w1' % nm][l].rearrange("(i d) h -> d i h", d=64)
            for hf in range(2):
                st, sk = self.stage.next()
                P.dma('sp', st[0:64, :].rearrange("p (i h) -> p i h", h=128), src[:, hf * 16:(hf + 1) * 16, :], [], [sk])
                P.v('dve', 'tensor_copy', [sk], ['w1_' + nm], w1[nm][:, hf * 16:(hf + 1) * 16, :], st[0:64, :].rearrange("p (i h) -> p i h", h=128))
            st, sk = self.stage.next()
            P.dma('sp', st[:, 0:64], pr['cmp_%s_w2' % nm][l], [], [sk])
            w2[nm] = s.sb("w2_" + nm, [128, 64], BF16)
            P.v('dve', 'tensor_copy', [sk], ['w2_' + nm], w2[nm][:], st[:, 0:64])
        posf = s.sb("posf", [64, 32], F32)
        P.dma('sp', posf[:], pr['cmp_pos'][l].rearrange("i d -> d i"), [], ['posf'], allow_slow_non_contiguous=True)
        posT = s.sb("posT", [64, 32], BF16)
        P.v('dve', 'tensor_copy', ['posf'], ['posT'], posT[:], posf[:])
        g_kc = self.load_col(s, "g_kc2", pr['kn_c'][l])
        pb = s.sb("pb", [128, 2], F32)
        pss = Rot(s.ps, "c_ps", [128, 512], F32, 3)
        accs = Rot(s.ps, "c_acc", [128, 4, 512], F32, 1)
        for wi, nm in enumerate(('k', 'v')):
            z, zk = pss.next()
            for i in range(32):
                P.mm(z[:, 0:1], w1[nm][:, i, :], posT[:, i:i + 1], i == 0, i == 31, ['w1_' + nm, 'posT'], [zk])
            P.v('dve', 'tensor_copy', [zk], ['pb'], pb[:, wi:wi + 1], z[:, 0:1])
        xc_in = Rot(s.sb, "xc_in", [64, S], BF16, 1)
        hid = Rot(s.sb, "hid", [128, 256], BF16, 2)
        kcn = s.sb("kcn", [64, 256], BF16)
        sqc = s.sb("sqc", [64, 256], BF16)
        rsc = s.sb("rsc", [64, 256], F32)
        vca = s.sb("vca", [128, 2, 128], BF16)
        qrot = Rot(s.sb, "c_qT", [128, 4, 512], BF16, 2)
        kwin = s.sb("c_kwin", [64, S], BF16)
        vsl = s.sb("c_vsl", [128, T, 65], BF16)
        vwn = s.sb("c_vwn", [128, T, 65], BF16)
        gts = s.sb("c_gts", [128, T, 12], F32)
        P.dma('sp', gts[:], self.gates.rearrange("p (t c) -> p t c", c=12), ['gates'], ['c_gts'])
        pT = Rot(s.sb, "c_pT", [128, 512], BF16, 3)
        imp = s.sb("c_imp", [128, 4, 64], F32)
        tmp = Rot(s.sb, "c_tmp", [128, 4, 64], F32, 2)
        rden = Rot(s.sb, "c_rd", [128, 4, 1], F32, 3)
        coef = Rot(s.sb, "c_coef", [128, 4, 1], F32, 3)
        sc = Rot(s.sb, "c_sc", [128, 64], F32, 2)
        sc2 = Rot(s.sb, "c_sc2", [128, 64], F32, 2)
        m8 = Rot(s.sb, "c_m8", [128, 8], F32, 2)
        nsel = Rot(s.sb, "c_nsel", [128, 64], BF16, 4)
        pstr = Rot(s.ps, "c_pstr", [128, 128], BF16, 1)
        ybuf = Rot(s.sb, "c_ybuf", [128, 4, 256], F32, 2)
        ident, causal, window, E, M0 = c['ident'], c['causal'], c['window'], c['E'], c['M0']

        for g in range(1):
            gs = slice(0, 64)
            P.v('pool', 'memset', [], ['vca'], vca[:], 0.0)
            for wi, (nm, src, skey) in enumerate((('k', self.kcT, 'kcT'), ('v', self.vcT, 'vcT'))):
                x_t, xk = xc_in.next()
                P.dma('sp', x_t[:], src[gs, :], [skey], [xk])
                z, zk = pss.next()
                for i in range(32):
                    P.mm(z[:, 0:n_cmp], w1[nm][:, i, :], x_t[:, i:i + 16 * (n_cmp - 1) + 1:16], i == 0, i == 31, ['w1_' + nm, xk], [zk])
                h_t, hk = hid.next()
                P.v('pool', 'memset', [], [hk], h_t[:], 0.0)
                P.act(h_t[:, 0:n_cmp], z[:, 0:n_cmp], AF.Silu, [zk, 'pb'], [hk], bias=pb[:, wi:wi + 1])
                if nm == 'k':
                    z2, z2k = pss.next()
                    P.mm(z2[0:64, 0:256], w2['k'][:], h_t[:], True, True, ['w2_k', hk], [z2k])
                    P.act(sqc[:], z2[0:64, 0:256], AF.Square, [z2k], ['sqc'])
                    z3, z3k = pss.next()
                    P.mm(z3[0:64, 0:256], c['bones'][0:64, 0:64], sqc[:], True, True, ['sqc', 'c_bones'], [z3k])
                    P.act(rsc[:], z3[0:64, 0:256], AF.Ln, [z3k], ['rsc'], scale=1.0 / 64, bias=EPS)
                    P.act(rsc[:], rsc[:], AF.Exp, ['rsc'], ['rsc'], scale=-0.5)
                    P.v('dve', 'scalar_tensor_tensor', [z2k, 'g_kc2', 'rsc'], ['kcn'], kcn[:], z2[0:64, 0:256], g_kc[0:64, 0:1], rsc[:], ALU.mult, ALU.mult)
                else:
                    for ct in range(NCT):
                        z2, z2k = pss.next()
                        P.mm(z2[:, 0:64], h_t[:, ct * 128:(ct + 1) * 128], w2['v'][:], True, True, [hk, 'w2_v'], [z2k])
                        P.v('dve', 'tensor_copy', [z2k], ['vca'], vca[:, ct, 0:64], z2[:, 0:64])
            for ct in range(2):
                P.v('pool', 'tensor_copy', ['c_overlap'], ['vca'], vca[:, ct, 64:128], c['overlap'][:, ct * 64:(ct + 1) * 64])
            P.dma('sp', E[0:64, :], self.kslcT[gs, :], ['kslcT'], ['c_E'])
            P.dma('sp', kwin[:], self.kwinT[gs, :], ['kwinT'], ['c_kwin'])
            P.dma('sp', vsl[:], self.vslc.rearrange("p (t c) -> p t c", c=65), ['vslc'], ['c_vsl'])
            P.dma('sp', vwn[:], self.vwin.rearrange("p (t c) -> p t c", c=65), ['vwin'], ['c_vwn'])
            for tc in range(NC):
                y_t, yk = ybuf.next()
                tsl = slice(tc * 512, (tc + 1) * 512)
                qTs, qTk = qrot.next()
                P.dma('sp', qTs[0:64], self.nsa_qT[4 * g:4 * g + 4, :, tsl].rearrange("h d n -> d h n"), ['nsa_qT'], [qTk])
                for r in range(4):
                    h = 4 * g + r
                    acc, acck = accs.next()
                    ktiles = []
                    cts = [ct for ct in range(NCT) if ct * 2048 + 31 < (tc + 1) * 512]
                    for ci, ct in enumerate(cts):
                        m_off = tc * 512 - ct * 2048
                        if m_off >= 0:
                            masks = [(ident[:], M0[:, m_off:m_off + 512], 0, 512, ['c_ident', 'c_M0'])]
                        else:
                            masks = []
                        qts = [0, 1, 2, 3]
                        ktiles.append(dict(lhsT=kcn[:, ct * 128:(ct + 1) * 128], keys=['kcn'], v=vca[:, ct, :], vkeys=['vca'],
                                           q0=0, masks=masks, qts=qts,
                                           first={qq: ci == 0 for qq in qts}, last={qq: ci == len(cts) - 1 for qq in qts}))
                    self.attn_chunk(qTs[0:64, r, :], qTk, ktiles, acc, acck, 128, pss, pT)
                    r_t, rk = rden.next()
                    P.v('dve', 'tensor_reduce', [acck], [rk], r_t[:].rearrange("p q o -> p (q o)"), acc[:, :, 64:128], AX.X, ALU.add)
                    P.v('dve', 'tensor_scalar', [rk], [rk], r_t[:], r_t[:], 1e-20, None, ALU.add)
                    P.v('dve', 'reciprocal', [rk], [rk], r_t[:], r_t[:])
                    c_t, ck = coef.next()
                    P.v('dve', 'tensor_tensor', [rk, 'c_gts'], [ck], c_t[:], r_t[:], gts[:, tc * 4:tc * 4 + 4, r:r + 1], ALU.mult)
                    P.v('dve', 'tensor_tensor', [acck, ck], [yk], y_t[:, :, r * 64:(r + 1) * 64], acc[:, :, 0:64], c_t[:].to_broadcast([128, 4, 64]), ALU.mult)
                    if r == 0:
                        P.v('dve', 'tensor_tensor', [acck, rk], ['c_imp'], imp[:], acc[:, :, 64:128], r_t[:].to_broadcast([128, 4, 64]), ALU.mult)
                    else:
                        t_t, tk = tmp.next()
                        P.v('dve', 'tensor_tensor', [acck, rk], [tk], t_t[:], acc[:, :, 64:128], r_t[:].to_broadcast([128, 4, 64]), ALU.mult)
                        P.v('dve', 'tensor_tensor', ['c_imp', tk], ['c_imp'], imp[:], imp[:], t_t[:], ALU.add)
                nsel_list = []
                for qq in range(4):
                    tt = tc * 4 + qq
                    s_t, sk = sc.next()
                    P.v('dve', 'tensor_tensor', ['c_imp', 'c_cand'], [sk], s_t[:], imp[:, qq, :], c['cand'][:, tt * 64:(tt + 1) * 64], ALU.mult)
                    P.v('dve', 'tensor_tensor', [sk, 'c_tbl2'], [sk], s_t[:], s_t[:], c['tbl2'][:, tt * 64:(tt + 1) * 64], ALU.add)
                    m_t, mk = m8.next()
                    s2_t, s2k = sc2.next()
                    P.v('dve', 'max', [sk], [mk], m_t[:], s_t[:])
                    P.v('dve', 'match_replace', [mk, sk], [s2k], s2_t[:], m_t[:], s_t[:], -2.0)
                    P.v('dve', 'max', [s2k], [mk], m_t[:], s2_t[:])
                    n_t, nk = nsel.next()
                    P.v('dve', 'tensor_scalar', [sk, mk], [s2k], s2_t[:], s_t[:], m_t[:, 7:8], -1.0, ALU.is_ge, ALU.add)
                    P.v('dve', 'tensor_scalar', [s2k], [nk], n_t[:], s2_t[:], BIG, None, ALU.mult)
                    nsel_list.append((n_t, nk, qq))
                for r in range(4):
                    h = 4 * g + r
                    acc, acck = accs.next()
                    ktiles = []
                    for kj in range(max(0, 4 * tc - 4), 4 * tc + 4):
                        qa = max(kj, 4 * tc) - 4 * tc
                        qb = min(kj + 4, 4 * tc + 3) - 4 * tc
                        masks = []
                        if kj >= 4 * tc:
                            masks.append((ident[:], causal[:], qa * 128, qa * 128 + 128, ['c_ident', 'c_causal']))
                        if kj + 4 <= 4 * tc + 3:
                            masks.append((ident[:], window[:], qb * 128, qb * 128 + 128, ['c_ident', 'c_window']))
                        qts = list(range(qa, qb + 1))
                        ktiles.append(dict(lhsT=kwin[:, kj * 128:(kj + 1) * 128], keys=['c_kwin'], v=vwn[:, kj, :], vkeys=['c_vwn'],
                                           q0=qa * 128, q1=(qb + 1) * 128, masks=masks, qts=qts,
                                           first={qq: kj == max(0, 4 * tc + qq - 4) for qq in qts},
                                           last={qq: kj == 4 * tc + qq for qq in qts}))
                    self.attn_chunk(qTs[0:64, r, :], qTk, ktiles, acc, acck, 65, pss, pT)
                    self.nsa_accum(acc, acck, y_t, yk, r, gts, tc, 8 + r, rden, coef, tmp)
                for (n_t, nk, qq) in nsel_list:
                    p_t, pk = pstr.next()
                    P.tr(p_t[64:128, :], n_t[:], ident[:], [nk, 'c_ident'], [pk])
                    P.v('dve', 'tensor_copy', [pk], [qTk], qTs[64:128, :, qq * 128:(qq + 1) * 128], p_t[64:128, :].unsqueeze(1).to_broadcast([64, 4, 128]))
                for r in range(4):
                    h = 4 * g + r
                    acc, acck = accs.next()
                    ktiles = []
                    for kj in range(4 * tc + 4):
                        diag = kj >= 4 * tc
                        qlo = max(kj, 4 * tc) - 4 * tc
                        masks = []
                        if diag:
                            masks.append((ident[:], causal[:], qlo * 128, qlo * 128 + 128, ['c_ident', 'c_causal']))
                        qts = list(range(qlo, 4))
                        ktiles.append(dict(lhsT=E[:, kj * 128:(kj + 1) * 128], keys=['c_E'], v=vsl[:, kj, :], vkeys=['c_vsl'],
                                           q0=qlo * 128, masks=masks, qts=qts,
                                           first={qq: kj == 0 for qq in qts}, last={qq: kj == 4 * tc + qq for qq in qts}))
                    self.attn_chunk(qTs[:, r, :], qTk, ktiles, acc, acck, 65, pss, pT)
                    self.nsa_accum(acc, acck, y_t, yk, r, gts, tc, 4 + r, rden, coef, tmp)
                P.dma('pool', self.y_c[tsl, :].rearrange("(q p) d -> p q d", p=128), y_t[:], [yk], [P.uk('y_c')])
        P.barrier()
        s.close()

    def nsa_accum(self, acc, acck, y_t, yk, r, gts, tc, gcol, rden, coef, tmp):
        P = self.P
        r_t, rk = rden.next()
        c_t, ck = coef.next()
        t_t, tk = tmp.next()
        P.v('dve', 'reciprocal', [acck], [rk], r_t[:], acc[:, :, 64:65])
        P.v('dve', 'tensor_tensor', [rk, 'c_gts'], [ck], c_t[:], r_t[:], gts[:, tc * 4:tc * 4 + 4, gcol:gcol + 1], ALU.mult)
        P.v('dve', 'tensor_tensor', [acck, ck], [tk], t_t[:], acc[:, :, 0:64], c_t[:].to_broadcast([128, 4, 64]), ALU.mult)
        P.v('dve', 'tensor_tensor', [yk, tk], [yk], y_t[:, :, r * 64:(r + 1) * 64], y_t[:, :, r * 64:(r + 1) * 64], t_t[:], ALU.add)

    def phase_G(self):
        P, S, T, NC = self.P, self.S, self.T, self.NC
        s = Scope(self.nc)
        yt = Rot(s.sb, "g_y", [128, WB], F32, 3)
        gt = Rot(s.sb, "g_g", [128, WB], F32, 3)
        yb = Rot(s.sb, "g_yb", [128, WB], BF16, 2)
        pst = Rot(s.ps, "g_ps", [128, 2, 128], BF16, 2)
        ob = Rot(s.sb, "g_ob", [128, 2, 512], BF16, 2)
        for (ysrc, yk0, gsrc, gk0, row0) in ((self.y_a, 'y_a', self.ga_s, 'ga_s', 0), (self.y_c, 'y_c', self.gc_s, 'gc_s', 2 * WB)):
            for tc in range(NC):
                o_t, ok = ob.next()
                for q in range(4):
                    tt = tc * 4 + q
                    y_t, yk = yt.next()
                    g_t, gk = gt.next()
                    b_t, bk = yb.next()
                    p_t, pk = pst.next()
                    P.dma('sp', y_t[:], ysrc[tt * 128:(tt + 1) * 128, :], [yk0], [yk])
                    P.dma('sp', g_t[:], gsrc[tt * 128:(tt + 1) * 128, :], [gk0], [gk])
                    P.v('dve', 'tensor_tensor', [yk, gk], [bk], b_t[:], y_t[:], g_t[:], ALU.mult)
                    for blk in range(2):
                        P.tr(p_t[:, blk, :], b_t[:, blk * 128:(blk + 1) * 128], self.c['ident'][:], [bk, 'c_ident'], [pk])
                    P.act(o_t[:, :, q * 128:(q + 1) * 128], p_t[:], AF.Copy, [pk], [ok])
                P.dma('pool', self.ysT[row0:row0 + WB, tc * 512:(tc + 1) * 512].rearrange("(b p) n -> p b n", p=128), o_t[:], [ok], [('ysT', row0 // WB)])
            n = row0 // WB
            self.allgather(self.ysT[n * WB:(n + 1) * WB, :], self.ysT_all[n * W:(n + 1) * W, :], [('ysT', n)], ['ysT_all'])
        P.barrier()
        s.close()

    def allgather(self, src, dst, r, w):
        self.P.add('pool', lambda e: e.collective_compute("AllGather", ALU.bypass, replica_groups=RG, ins=[src], outs=[dst]), r, w, cc=True)

    def phase_M(self, xin, xout):
        P, S, T, NC, l, L = self.P, self.S, self.T, self.NC, self.l, self.L
        s = Scope(self.nc)
        ys = Rot(s.sb, "m_ys", [128, 16, 512], BF16, 2)
        wmg = Rot(s.sb, "m_wmg", [128, 8, 4, 128], BF16, 2)
        wbr = Rot(s.sb, "m_wbr", [128, 16, 128], BF16, 2)
        psg = Rot(s.ps, "m_psg", [128, 512], F32, 4)
        psp = Rot(s.ps, "m_psp", [128, 512], F32, 4)
        sig = Rot(s.sb, "m_sig", [128, 512], F32, 3)
        macc = Rot(s.sb, "m_acc", [128, 512], F32, 2)
        mtmp = Rot(s.sb, "m_tmp", [128, 512], F32, 3)
        mT = Rot(s.sb, "m_mT", [128, 4, 512], BF16, 2)
        for tc in range(NC):
            tsl = slice(tc * 512, (tc + 1) * 512)
            y_t, yk = ys.next()
            P.dma('sp', y_t[:], self.ysT_all[:, tsl].rearrange("(c p) n -> p c n", p=128), ['ysT_all'], [yk])
            m_t, mk = mT.next()
            for dc in range(4):
                wm_t, wmk = wmg.next()
                wb_t, wbk = wbr.next()
                for n in range(4):
                    c0 = L_MG + n * W + dc * 128
                    P.dma('sp', wm_t[:, :, n, :], self.w_in_b[:, c0:c0 + 128].rearrange("(kc p) d -> p kc d", p=128), ['w_in_b'], [wmk])
                P.dma('sp', wb_t[:], self.w_br_b[:, dc * 128:(dc + 1) * 128].rearrange("(c p) d -> p c d", p=128), ['w_br_b'], [wbk])
                a_t, ak = macc.next()
                for n in range(4):
                    g_ps, gk = psg.next()
                    for kc in range(8):
                        P.mm(g_ps[:, 0:512], wm_t[:, kc, n, :], self.xnT[:, kc, tsl], kc == 0, kc == 7, [wmk, 'xnT'], [gk])
                    p_ps, pk = psp.next()
                    chunks = [n * 4 + k for k in range(4)]
                    for ci, c in enumerate(chunks):
                        P.mm(p_ps[:, 0:512], wb_t[:, c, :], y_t[:, c, :], ci == 0, ci == 3, [wbk, yk], [pk])
                    s_t, sk = sig.next()
                    P.act(s_t[:], g_ps[:, 0:512], AF.Sigmoid, [gk], [sk])
                    if n == 0:
                        P.v('dve', 'tensor_tensor', [pk, sk], [ak], a_t[:], p_ps[:, 0:512], s_t[:], ALU.mult)
                    else:
                        t_t, tk = mtmp.next()
                        P.v('dve', 'tensor_tensor', [pk, sk], [tk], t_t[:], p_ps[:, 0:512], s_t[:], ALU.mult)
                        if n < 3:
                            P.v('pool', 'tensor_tensor', [ak, tk], [ak], a_t[:], a_t[:], t_t[:], ALU.add)
                        else:
                            P.v('pool', 'tensor_tensor', [ak, tk], [mk], m_t[:, dc, :], a_t[:], t_t[:], ALU.add)
            P.dma('pool', self.mT[:, tsl].rearrange("(c p) n -> p c n", p=128), m_t[:], [mk], ['mT'])
        for j in range(2):
            self.allgather(self.mT[j * WB:(j + 1) * WB, :], self.mT_all[j * W:(j + 1) * W, :], ['mT'], ['mT_all'])
        P.barrier()
        s.close()
        s = Scope(self.nc)
        wo = s.sb("m_wo", [128, 8, W], BF16)
        P.dma('sp', wo[:], self.w_out_b.rearrange("(kc p) n -> p kc n", p=128), ['w_out_b'], ['m_wo'])
        mA = Rot(s.sb, "m_mA", [128, 8, 512], BF16, 2)
        pso = Rot(s.ps, "m_pso", [128, 512], F32, 2)
        xt = Rot(s.sb, "m_x", [128, W], F32, 3)
        ot = Rot(s.sb, "m_o", [128, W], F32, 3)
        last = (l == L - 1)
        xsrc = self.xm if l == 0 else self.xnew
        dst = self.y if last else self.xnew
        for tc in range(NC):
            tsl = slice(tc * 512, (tc + 1) * 512)
            m_t, mk = mA.next()
            P.dma('sp', m_t[:], self.mT_all[:, tsl].rearrange("(c p) n -> p c n", p=128), ['mT_all'], [mk])
            for q in range(4):
                tt = tc * 4 + q
                x_t, xk = xt.next()
                o_t, ok = ot.next()
                P.dma('sp', x_t[:], xsrc[tt * 128:(tt + 1) * 128, :], [] if l == 0 else [('xnew', tt)], [xk])
                o_ps, opk = pso.next()
                for dc in range(8):
                    P.mm(o_ps[:, 0:512], m_t[:, dc, q * 128:(q + 1) * 128], wo[:, dc, :], dc == 0, dc == 7, [mk, 'm_wo'], [opk])
                P.v('dve', 'tensor_tensor', [opk, xk], [ok], o_t[:], o_ps[:, 0:512], x_t[:], ALU.add)
                P.dma('pool', dst[tt * 128:(tt + 1) * 128, :], o_t[:], [ok], ['y_out'] if last else [('xnew', tt), ('xnew_blk', tt // 8)])
                if not last and tt % 8 == 7:
                    k = tt // 8
                    self.allgather(self.xnew[k * 1024:(k + 1) * 1024, :], self.xres[2 * k * 1024:(2 * k + 2) * 1024, :], [('xnew_blk', k)], ['xres'])
        P.barrier()
        s.close()

_CACHE = {}


def get_builder(S, L, dbg=(), stop_after=None):
    key = (S, L, tuple(dbg), stop_after)
    if key not in _CACHE:
        b = Builder(S, L, dbg, stop_after)
        b.build()
        _CACHE[key] = b
    return _CACHE[key]


def make_in_maps(b, inputs, core_ids):
    x = np.asarray(inputs["x"], dtype=np.float32)
    halves = [slice_params(inputs, h) for h in range(2)]
    maps = []
    for c in core_ids:
        bi, h = c // 2, c % 2
        m = {"x": np.ascontiguousarray(x[bi]), "xm": np.ascontiguousarray(x[bi][:, h * W:(h + 1) * W]),
             "consts": b.consts_np}
        m.update(halves[h])
        maps.append(m)
    return maps


def kernel(**inputs):
    x = np.asarray(inputs["x"])
    B, S, _ = x.shape
    L = np.asarray(inputs["norm_g"]).shape[0]
    b = get_builder(S, L)
    inputs = {k: np.asarray(v) for k, v in inputs.items()}
    n_cores = 2 * B
    maps = make_in_maps(b, inputs, list(range(n_cores)))
    res = run_bass_kernel_spmd(b.nc, maps, core_ids=list(range(n_cores)))
    out = np.empty((B, S, D), np.float32)
    for c in range(n_cores):
        out[c // 2][:, (c % 2) * W:(c % 2 + 1) * W] = res.results[c]["y"]
    return out
```

```python
import math
from contextlib import ExitStack
import numpy as np
import concourse.bass as bass
import concourse.mybir as mybir
from concourse.bass_utils import run_bass_kernel_spmd

F32 = mybir.dt.float32
BF16 = mybir.dt.bfloat16
AF = mybir.ActivationFunctionType
ALU = mybir.AluOpType
AX = mybir.AxisListType

D = 1024
W = 512
WIN = 10528
BIG = 30000.0
EPS = 1e-6
O_QA, O_KA, O_VA, O_FA, O_GA = 0, 512, 1024, 1536, 1544
O_XB, O_GB = 2056, 2568
O_QC, O_KCC, O_VCC, O_KSC, O_VSC, O_KWC, O_VWC, O_GATE, O_GC = 3080, 3592, 3720, 3848, 3976, 4104, 4232, 4360, 4384
O_UD, O_VD, O_GD, O_MG = 4896, 5408, 5920, 6432
WB = 256
NH = 4
L_QA, L_KA, L_VA, L_FA, L_GA, L_XB, L_GB, L_QC = 0, 256, 512, 768, 772, 1028, 1284, 1540
L_KCC, L_VCC, L_KSC, L_KWC, L_VSC, L_VWC, L_GATE, L_GC = 1796, 1860, 1924, 1988, 2052, 2116, 2180, 2192
L_UD, L_VD, L_GD, L_MG, WM = 2448, 2704, 3216, 3472, 5520
RG = [[0, 1], [2, 3], [4, 5], [6, 7]]


def local_cols(h):
    r = lambda o, n: list(range(o, o + n))
    cols = []
    cols += r(O_QA + h * 256, 256) + r(O_KA + h * 256, 256) + r(O_VA + h * 256, 256) + r(O_FA + h * 4, 4)
    cols += r(O_GA + h * 256, 256) + r(O_XB + h * 256, 256) + r(O_GB + h * 256, 256) + r(O_QC + h * 256, 256)
    cols += r(O_KCC + h * 64, 64) + r(O_VCC + h * 64, 64) + r(O_KSC + h * 64, 64) + r(O_KWC + h * 64, 64)
    cols += r(O_VSC + h * 64, 64) + r(O_VWC + h * 64, 64)
    for br in range(3):
        cols += r(O_GATE + br * 8 + h * 4, 4)
    cols += r(O_GC + h * 256, 256) + r(O_UD + h * 256, 256)
    cols += r(O_VD + h * 256, 256) + r(O_VD + (1 - h) * 256, 256)
    cols += r(O_GD + h * 256, 256)
    for n in range(4):
        cols += r(O_MG + n * D + h * 512, 512)
    assert len(cols) == WM
    return np.asarray(cols)


class Prog:
    NDMA = 8

    def __init__(self, nc):
        self.nc = nc
        self.ops = []

    def add(self, q, fn, r=(), w=(), dma=False, barrier=False, cc=False):
        self.ops.append(dict(q=q, fn=fn, r=tuple(r), w=tuple(w), dma=dma or cc, barrier=barrier, cc=cc))

    def mm(self, out, lhsT, rhs, start, stop, r, w):
        self.add('pe', lambda e: e.matmul(out, lhsT, rhs, start=start, stop=stop), r, w)

    def tr(self, out, in_, ident, r, w):
        self.add('pe', lambda e: e.transpose(out, in_, ident), r, w)

    def act(self, out, in_, func, r, w, bias=None, scale=1.0, accum_out=None):
        def f(e):
            kw = {}
            if bias is not None:
                kw['bias'] = bias
            if accum_out is not None:
                kw['accum_out'] = accum_out
            return e.activation(out, in_, func, scale=scale, **kw)
        self.add('act', f, r, w)

    def v(self, q, name, r, w, *args, **kw):
        self.add(q, lambda e: getattr(e, name)(*args, **kw), r, w)

    def dma(self, q, out, in_, r, w, **kw):
        self.add(q, lambda e: e.dma_start(out, in_, **kw), r, w, dma=True)

    UK = [0]

    def uk(self, name):
        Prog.UK[0] += 1
        return (name, 'u', Prog.UK[0])

    def barrier(self):
        for q in ('pe', 'act', 'dve', 'pool', 'sp'):
            self.add(q, None, barrier=True)

    def finalize(self):
        nc = self.nc
        ops = self.ops
        last_w, rd_c, rd_d = {}, {}, {}
        last_q = {}
        recent_dma = {}
        for i, op in enumerate(ops):
            deps = set()
            if op['barrier']:
                for q, j in last_q.items():
                    deps.add(j)
                for q, lst in recent_dma.items():
                    deps.update(lst)
            for k in op['r']:
                if k in last_w:
                    deps.add(last_w[k])
            for k in op['w']:
                if k in last_w:
                    deps.add(last_w[k])
                deps.update(rd_c.get(k, {}).values())
                deps.update(rd_d.get(k, ()))
            deps.discard(i)
            op['deps'] = deps
            for k in op['w']:
                last_w[k] = i
                rd_c[k] = {}
                rd_d[k] = []
            for k in op['r']:
                if k in op['w']:
                    continue
                if op['dma']:
                    rd_d.setdefault(k, []).append(i)
                else:
                    rd_c.setdefault(k, {})[op['q']] = i
            if op['fn'] is not None:
                if op.get('cc'):
                    pass
                elif op['dma']:
                    lst = recent_dma.setdefault(op['q'], [])
                    lst.append(i)
                    if len(lst) > self.NDMA:
                        lst.pop(0)
                else:
                    last_q[op['q']] = i
        needed = [False] * len(ops)
        for op in ops:
            for j in op['deps']:
                needed[j] = True
        queues = sorted(set(op['q'] for op in ops))
        csem = {q: nc.alloc_semaphore('c_' + q) for q in queues}
        dsem = {q: [nc.alloc_semaphore('d_%s%d' % (q, k)) for k in range(self.NDMA)]
                for q in queues if any(o['dma'] and o['q'] == q for o in ops)}
        ccount = {q: 0 for q in queues}
        dcount = {q: 0 for q in queues}
        for i, op in enumerate(ops):
            q = op['q']
            op['sem'] = None
            op['pre'] = None
            if op['fn'] is None:
                continue
            if op.get('cc'):
                if 'cc' not in ccount:
                    ccount['cc'] = 0
                    self.ccsem = nc.alloc_semaphore('cc_sem')
                ccount['cc'] += 1
                op['sem'] = self.ccsem
                op['tick'] = ccount['cc']
                op['inc'] = None
                if ccount['cc'] > 1:
                    op['pre'] = (self.ccsem, ccount['cc'] - 1)
            elif op['dma']:
                n = dcount[q]
                dcount[q] += 1
                op['sem'] = dsem[q][n % self.NDMA]
                op['tick'] = 16 * (n // self.NDMA + 1)
                if n >= self.NDMA:
                    op['pre'] = (op['sem'], 16 * (n // self.NDMA))
                op['inc'] = 16
            elif needed[i]:
                ccount[q] += 1
                op['sem'] = csem[q]
                op['tick'] = ccount[q]
                op['inc'] = 1
        by_q = {q: [i for i, o in enumerate(ops) if o['q'] == q] for q in queues}
        self.n_waits = 0

        def run(q, eng):
            waited = {}
            for i in by_q[q]:
                op = ops[i]
                waits = []
                if op['pre'] is not None:
                    waits.append(op['pre'])
                for j in sorted(op['deps']):
                    dj = ops[j]
                    if dj['sem'] is None:
                        continue
                    if dj['q'] == q and q == 'pe' and not dj['dma']:
                        continue
                    waits.append((dj['sem'], dj['tick']))
                for sem, val in waits:
                    key = id(sem)
                    if waited.get(key, 0) >= val:
                        continue
                    waited[key] = val
                    eng.wait_ge(sem, val)
                    self.n_waits += 1
                if op['fn'] is None:
                    continue
                ins = op['fn'](eng)
                if op['sem'] is not None:
                    if op['inc'] is None:
                        ins.then_inc(op['sem'])
                    else:
                        ins.then_inc(op['sem'], op['inc'])

        emap = {'pe': 'tensor', 'act': 'scalar', 'dve': 'vector', 'pool': 'gpsimd', 'sp': 'sync'}
        with nc.Block() as block:
            for q in queues:
                getattr(block, emap[q])(lambda eng, q=q: run(q, eng))


class Rot:
    def __init__(self, alloc, name, shape, dtype, n):
        self.t = [alloc(name + str(i), shape, dtype) for i in range(n)]
        self.name = name
        self.i = 0

    def next(self):
        k = self.i % len(self.t)
        self.i += 1
        return self.t[k], (self.name, k)


class Scope:
    def __init__(self, nc):
        self.nc = nc
        self.st = ExitStack()

    UID = [0]

    def sb(self, name, shape, dt):
        Scope.UID[0] += 1
        return self.st.enter_context(self.nc.sbuf_tensor("%s_%d" % (name, Scope.UID[0]), shape, dt))

    def ps(self, name, shape, dt=F32):
        Scope.UID[0] += 1
        return self.st.enter_context(self.nc.psum_tensor("%s_%d" % (name, Scope.UID[0]), shape, dt))

    def close(self):
        self.st.close()


def host_consts(S):
    T = S // 128
    n_slc = S // 64
    p = np.arange(128)
    c = {}
    c['ident'] = np.eye(128, dtype=np.float32)
    c['bones'] = (p[:, None] // 64 == p[None, :] // 64).astype(np.float32)
    c['causal'] = np.where(p[:, None] > p[None, :], -BIG, 0.0).astype(np.float32)
    c['window'] = np.where(p[:, None] <= p[None, :], -BIG, 0.0).astype(np.float32)
    c['tril'] = (p[None, :] <= p[:, None]).astype(np.float32)
    E = np.zeros((128, T * 128), np.float32)
    for kj in range(T):
        for half in range(2):
            j = 2 * kj + half
            if j < 64:
                E[64 + j, kj * 128 + half * 64: kj * 128 + half * 64 + 64] = 1.0
    c['E'] = E
    u = np.arange(S)
    c['M0'] = np.where(u[None, :] >= 16 * p[:, None] + 31, 0.0, -BIG).astype(np.float32)
    n_cmp = (S - 32) // 16 + 1
    c0 = np.arange(256) * 16
    s0 = np.arange(64) * 64
    ov = np.minimum(c0[:, None] + 32, s0[None, :] + 64) - np.maximum(c0[:, None], s0[None, :])
    ov = (np.clip(ov, 0, None) / 32.0).astype(np.float32)
    ov[n_cmp:] = 0.0
    ov[:, n_slc:] = 0.0
    c['overlap'] = ov.reshape(2, 128, 64).transpose(1, 0, 2).reshape(128, 128)
    t = np.arange(S)
    jt = t // 64
    j = np.arange(64)
    valid = j[None, :] <= jt[:, None]
    forced = (j[None, :] == 0) | (valid & (j[None, :] > jt[:, None] - 2))
    cand = valid & ~forced
    tbl2 = np.where(forced, 1e6, np.where(cand, 0.0, -1.0)).astype(np.float32)
    if n_slc < 64:
        tbl2[:, n_slc:] = -1.0
    c['cand'] = cand.astype(np.float32).reshape(T, 128, 64).transpose(1, 0, 2).reshape(128, T * 64)
    c['tbl2'] = tbl2.reshape(T, 128, 64).transpose(1, 0, 2).reshape(128, T * 64)
    names = ['ident', 'bones', 'causal', 'window', 'tril', 'E', 'M0', 'overlap', 'cand', 'tbl2']
    offs = {}
    o = 0
    for n in names:
        offs[n] = (o, c[n].shape[1])
        o += c[n].shape[1]
    return np.concatenate([c[n] for n in names], axis=1), offs


PARAMS = [("norm_g", [D]), ("w_in", [D, WM]), ("b_forget", [NH]), ("qn_a", [64]), ("kn_a", [64]),
          ("conv_w", [4, WB]), ("conv_b", [WB]), ("w_rg_a", [4, 64, 64]), ("b_rg_a", [WB]),
          ("w_rg_x", [4, 64, 64]), ("b_rg_x", [WB]), ("lru_lambda", [WB]), ("qn_c", [64]),
          ("kn_c", [64]), ("cmp_pos", [32, 64]), ("cmp_k_w1", [2048, 128]), ("cmp_k_w2", [128, 64]),
          ("cmp_v_w1", [2048, 128]), ("cmp_v_w2", [128, 64]), ("ln_v_g", [W]),
          ("w_spatial", [4, 128, 128]), ("b_spatial", [4, 128]), ("w_branch", [4 * W, W]),
          ("w_out", [D, W])]


def slice_params(inputs, h):
    a = lambda k: np.asarray(inputs[k], dtype=np.float32)
    hs = slice(h * WB, (h + 1) * WB)
    p = {}
    p["norm_g"] = a("norm_g")
    p["w_in"] = a("w_in")[:, :, local_cols(h)]
    p["b_forget"] = a("b_forget")[:, h * NH:(h + 1) * NH]
    for k in ("qn_a", "kn_a", "qn_c", "kn_c", "cmp_pos", "cmp_k_w1", "cmp_k_w2", "cmp_v_w1", "cmp_v_w2"):
        p[k] = a(k)
    p["conv_w"] = a("conv_w")[:, :, hs]
    for k in ("conv_b", "b_rg_a", "b_rg_x", "lru_lambda"):
        p[k] = a(k)[:, hs]
    p["w_rg_a"] = a("w_rg_a")[:, h * 4:(h + 1) * 4]
    p["w_rg_x"] = a("w_rg_x")[:, h * 4:(h + 1) * 4]
    lg = a("ln_v_g")
    p["ln_v_g"] = np.concatenate([lg[:, hs], lg[:, (1 - h) * WB:(2 - h) * WB]], axis=1)
    p["w_spatial"] = a("w_spatial")[:, h * 4:(h + 1) * 4]
    p["b_spatial"] = a("b_spatial")[:, h * 4:(h + 1) * 4]
    wb = a("w_branch")
    Lr = wb.shape[0]
    p["w_branch"] = wb.reshape(Lr, 4 * W, D)[:, :, h * W:(h + 1) * W]
    wo = a("w_out")[:, :, h * W:(h + 1) * W]
    p["w_out"] = wo.reshape(Lr, 2, 2, WB, W).transpose(0, 2, 1, 3, 4).reshape(Lr, D, W)
    return {k: np.ascontiguousarray(v) for k, v in p.items()}


class Builder:
    def __init__(self, S, L, dbg=(), stop_after=None):
        self.S, self.L = S, L
        self.T = S // 128
        self.NC = S // 512
        self.dbg = set(dbg)
        self.stop_after = stop_after
        self.nc = bass.Bass("TRN2", target_bir_lowering=False)
        self.P = Prog(self.nc)
        self.consts_np, self.coffs = host_consts(S)

    def din(self, name, shape, dt=F32):
        return self.nc.dram_tensor(name, list(shape), dt, kind="ExternalInput").ap()

    def dscr(self, name, shape, dt):
        kind = "ExternalOutput" if name in self.dbg else "Internal"
        return self.nc.dram_tensor(name, list(shape), dt, kind=kind).ap()

    def build(self):
        nc, P, S, L, T = self.nc, self.P, self.S, self.L, self.T
        self.x = self.din("x", [S, D])
        self.xm = self.din("xm", [S, W])
        self.y = self.nc.dram_tensor("y", [S, W], F32, kind="ExternalOutput").ap()
        self.pr = {n: self.din(n, [L] + shp) for n, shp in PARAMS}
        self.cst = self.din("consts", list(self.consts_np.shape))
        d = self.dscr
        self.xnew = d("xnew", [S, W], F32)
        self.xres = d("xres", [2 * S, W], F32)
        self.w_in_b = d("w_in_b", [D, WM], BF16)
        self.w_br_b = d("w_br_b", [4 * W, W], BF16)
        self.w_out_b = d("w_out_b", [D, W], BF16)
        self.fox_qT = d("fox_qT", [NH, 70, S], BF16)
        self.fox_kT = d("fox_kT", [NH, 70, S], BF16)
        self.fox_v = d("fox_v", [S, NH, 65], BF16)
        self.ga_s = d("ga_s", [S, WB], F32)
        self.gc_s = d("gc_s", [S, WB], F32)
        self.y_a = d("y_a", [S, WB], F32)
        self.y_c = d("y_c", [S, WB], F32)
        self.nsa_qT = d("nsa_qT", [NH, 64, S], BF16)
        self.kcT = d("kcT", [64, S], BF16)
        self.vcT = d("vcT", [64, S], BF16)
        self.kslcT = d("kslcT", [64, S], BF16)
        self.kwinT = d("kwinT", [64, S], BF16)
        self.vslc = d("vslc", [128, (S // 128) * 65], BF16)
        self.vwin = d("vwin", [128, (S // 128) * 65], BF16)
        self.gates = d("gates", [128, (S // 128) * 12], F32)
        self.ysT = d("ysT", [4 * WB, S], BF16)
        self.ysT_all = d("ysT_all", [8 * WB, S], BF16)
        self.mT = d("mT", [W, S], BF16)
        self.mT_all = d("mT_all", [D, S], BF16)

        g = Scope(nc)
        self.g = g
        self.xnT = g.sb("xnT", [128, 8, S], BF16)
        self.c = {}
        gnames = ['ident', 'bones', 'causal', 'window', 'tril', 'overlap']
        self.alloc_consts(g, gnames)
        self.ones_f = g.sb("ones_f", [128, 1], F32)
        P.v('pool', 'memset', [], ['ones_f'], self.ones_f[:], 1.0)
        s0 = Scope(nc)
        self.stage = Rot(s0.sb, "stage", [128, 2048], F32, 2)
        self.load_consts(gnames)
        P.barrier()
        s0.close()
        for l in range(L):
            self.l = l
            xin = None
            xout = None
            if self.stop_after == 'C0':
                break
            self.phase_W()
            if self.stop_after == 'W':
                break
            self.phase_N(xin)
            if self.stop_after == 'N':
                break
            self.phase_J()
            if self.stop_after == 'J':
                break
            self.phase_A()
            if self.stop_after == 'A':
                break
            self.phase_C()
            if self.stop_after == 'C':
                break
            self.phase_G()
            if self.stop_after == 'G':
                break
            self.phase_M(xin, xout)
        P.barrier()
        P.finalize()
        g.close()
        return nc

    def alloc_consts(self, g, names):
        for n in names:
            o, w = self.coffs[n]
            self.c[n] = g.sb("c_" + n, [128, w], BF16)

    def load_consts(self, names):
        P = self.P
        for n in names:
            o, w = self.coffs[n]
            t = self.c[n]
            for c0 in range(0, w, 2048):
                cw = min(2048, w - c0)
                st, sk = self.stage.next()
                P.dma('sp', st[:, 0:cw], self.cst[:, o + c0:o + c0 + cw], [], [sk])
                P.v('dve', 'tensor_copy', [sk], ['c_' + n], t[:, c0:c0 + cw], st[:, 0:cw])

    def cast_dram(self, src, dst, R, C, dkey, cnt):
        P = self.P
        engs = ['dve', 'act']
        for r0 in range(0, R, 128):
            for c0 in range(0, C, 2048):
                cw = min(2048, C - c0)
                st, sk = self.stage.next()
                bt, bk = self.castb.next()
                P.dma('sp', st[:, 0:cw], src[r0:r0 + 128, c0:c0 + cw], [], [sk])
                e = engs[cnt[0] % 2]
                cnt[0] += 1
                if e == 'act':
                    P.act(bt[:, 0:cw], st[:, 0:cw], AF.Copy, [sk], [bk])
                else:
                    P.v(e, 'tensor_copy', [sk], [bk], bt[:, 0:cw], st[:, 0:cw])
                P.dma('pool', dst[r0:r0 + 128, c0:c0 + cw], bt[:, 0:cw], [bk], [P.uk(dkey)])

    def phase_W(self):
        P, l = self.P, self.l
        s = Scope(self.nc)
        self.stage = Rot(s.sb, "stage", [128, 2048], F32, 2)
        self.castb = Rot(s.sb, "castb", [128, 2048], BF16, 3)
        cnt = [0]
        self.cast_dram(self.pr['w_in'][l], self.w_in_b, D, WM, 'w_in_b', cnt)
        self.cast_dram(self.pr['w_branch'][l], self.w_br_b, 4 * W, W, 'w_br_b', cnt)
        self.cast_dram(self.pr['w_out'][l], self.w_out_b, D, W, 'w_out_b', cnt)
        P.barrier()
        s.close()

    def phase_N(self, xin):
        P, S, T, l = self.P, self.S, self.T, self.l
        s = Scope(self.nc)
        gb = s.sb("gb", [128, D], F32)
        P.dma('sp', gb[:], self.pr['norm_g'][l].partition_broadcast(128), [], ['gb'])
        xt = Rot(s.sb, "xt", [128, 4, D], F32, 2)
        junk = Rot(s.sb, "junk", [128, D], BF16, 2)
        ss = Rot(s.sb, "ss", [128, 4], F32, 2)
        rs = Rot(s.sb, "rsn", [128, 4], F32, 2)
        xn = Rot(s.sb, "xn", [128, D], BF16, 3)
        pst = Rot(s.ps, "pst", [128, 8, 128], BF16, 3)
        ident = self.c['ident']
        for g in range(T // 4):
            x_t, xk = xt.next()
            s_t, sk = ss.next()
            r_t, rk = rs.next()
            for q in range(4):
                t = g * 4 + q
                if self.l == 0:
                    P.dma('sp', x_t[:, q, :], self.x[t * 128:(t + 1) * 128, :], [], [(xk, q)])
                else:
                    for hf in range(2):
                        r0 = ((t // 8) * 2 + hf) * 1024 + (t % 8) * 128
                        P.dma('sp', x_t[:, q, hf * W:(hf + 1) * W], self.xres[r0:r0 + 128, :], ['xres'], [(xk, q, hf)])
            for q in range(4):
                j_t, jk = junk.next()
                rk_x = [(xk, q)] if self.l == 0 else [(xk, q, 0), (xk, q, 1)]
                P.act(j_t[:], x_t[:, q, :], AF.Square, rk_x, [jk, sk], accum_out=s_t[:, q:q + 1])
            P.act(r_t[:], s_t[:], AF.Ln, [sk], [rk], scale=1.0 / D, bias=EPS)
            P.act(r_t[:], r_t[:], AF.Exp, [rk], [rk], scale=-0.5)
            for q in range(4):
                t = g * 4 + q
                n_t, nk = xn.next()
                p_t, pk = pst.next()
                rk_x = [(xk, q)] if self.l == 0 else [(xk, q, 0), (xk, q, 1)]
                P.v('dve', 'scalar_tensor_tensor', rk_x + [rk, 'gb'], [nk], n_t[:], x_t[:, q, :], r_t[:, q:q + 1], gb[:], ALU.mult, ALU.mult)
                for kc in range(8):
                    P.tr(p_t[:, kc, :], n_t[:, kc * 128:(kc + 1) * 128], ident[:], [nk, 'c_ident'], [pk])
                if q % 2 == 0:
                    P.v('dve', 'tensor_copy', [pk], ['xnT'], self.xnT[:, :, t * 128:(t + 1) * 128], p_t[:])
                else:
                    P.act(self.xnT[:, :, t * 128:(t + 1) * 128], p_t[:], AF.Copy, [pk], ['xnT'])
        P.barrier()
        s.close()

    def load_w(self, c0, n):
        wt, wk = self.wrot.next()
        self.P.dma('sp', wt[:, :, 0:n], self.w_in_b[:, c0:c0 + n].rearrange("(kc p) n -> p kc n", p=128), ['w_in_b'], [wk])
        return wt, wk

    def fm(self, wt, wk, m0, m, tc, ps, pk):
        for kc in range(8):
            self.P.mm(ps[0:m, 0:512], wt[:, kc, m0:m0 + m], self.xnT[:, kc, tc * 512:(tc + 1) * 512], kc == 0, kc == 7, [wk, 'xnT'], [pk])

    def tm(self, wt, wk, n0, n, tt, ps, pk):
        for kc in range(8):
            self.P.mm(ps[:, 0:n], self.xnT[:, kc, tt * 128:(tt + 1) * 128], wt[:, kc, n0:n0 + n], kc == 0, kc == 7, [wk, 'xnT'], [pk])

    def load_col(self, s, name, src, scale=None):
        t = s.sb(name, [128, 1], F32)
        for h in range(2):
            self.P.dma('sp', t[h * 64:(h + 1) * 64, :], src.rearrange("(p o) -> p o", o=1), [], [name])
        if scale is not None:
            self.P.v('dve', 'tensor_scalar', [name], [name], t[:], t[:], float(scale), None, ALU.mult)
        return t

    def norm_epilogue2(self, z, zk, sq, sqk, gcol, gk, out, okeys):
        P = self.P
        ps2, p2k = self.ps2rot.next()
        rs, rsk = self.rsrot.next()
        P.mm(ps2[:, 0:512], self.c['bones'][:], sq[:], True, True, [sqk, 'c_bones'], [p2k])
        P.act(rs[:], ps2[:, 0:512], AF.Ln, [p2k], [rsk], scale=1.0 / 64, bias=EPS)
        P.act(rs[:], rs[:], AF.Exp, [rsk], [rsk], scale=-0.5)
        P.v('dve', 'scalar_tensor_tensor', [zk, gk, rsk], okeys, out, z[:, 0:512], gcol[:, 0:1], rs[:], ALU.mult, ALU.mult)

    def norm_epilogue(self, z, zk, gcol, gk, out, okeys):
        P = self.P
        sq, sqk = self.sqrot.next()
        ps2, p2k = self.ps2rot.next()
        rs, rsk = self.rsrot.next()
        P.act(sq[:], z[:, 0:512], AF.Square, [zk], [sqk])
        P.mm(ps2[:, 0:512], self.c['bones'][:], sq[:], True, True, [sqk, 'c_bones'], [p2k])
        P.act(rs[:], ps2[:, 0:512], AF.Ln, [p2k], [rsk], scale=1.0 / 64, bias=EPS)
        P.act(rs[:], rs[:], AF.Exp, [rsk], [rsk], scale=-0.5)
        P.v('dve', 'scalar_tensor_tensor', [zk, gk, rsk], okeys, out, z[:, 0:512], gcol[:, 0:1], rs[:], ALU.mult, ALU.mult)

    def phase_J(self):
        P = self.P
        s = Scope(self.nc)
        self.wrot = Rot(s.sb, "wt", [128, 8, 512], BF16, 3)
        psz = Rot(s.ps, "psz", [128, 512], F32, 4)
        import os
        stj = int(os.environ.get("STOPJ", "9"))
        if stj >= 1:
            self.phase_J_norm(psz)
        if stj >= 2:
            self.phase_J_forget(psz)
        if stj >= 3:
            self.phase_J_tok(psz)
        if stj >= 4:
            self.phase_J_B(psz)
        if stj >= 5:
            self.phase_J_D(psz)
        P.barrier()
        s.close()

    def phase_J_norm(self, psz):
        P, S, T, NC, l = self.P, self.S, self.T, self.NC, self.l
        pr = self.pr
        s = Scope(self.nc)
        self.sqrot = Rot(s.sb, "sq", [128, 512], BF16, 3)
        self.rsrot = Rot(s.sb, "rs", [128, 512], F32, 2)
        self.ps2rot = Rot(s.ps, "ps2", [128, 512], F32, 2)
        outT = Rot(s.sb, "outT", [128, S], BF16, 2)
        g_qa = self.load_col(s, "g_qa", pr['qn_a'][l], 0.125)
        g_ka = self.load_col(s, "g_ka", pr['kn_a'][l])
        g_qc = self.load_col(s, "g_qc", pr['qn_c'][l], 0.125)
        g_kc = self.load_col(s, "g_kc", pr['kn_c'][l])

        def normed_block(c0, gcol, gk, dsts):
            wt, wk = self.load_w(c0, 128)
            o_t, ok = outT.next()
            pend = None
            for tc in range(NC):
                z, zk = psz.next()
                self.fm(wt, wk, 0, 128, tc, z, zk)
                sq, sqk = self.sqrot.next()
                P.act(sq[:], z[:, 0:512], AF.Square, [zk], [sqk])
                if pend is not None:
                    self.norm_epilogue2(*pend)
                pend = (z, zk, sq, sqk, gcol, gk, o_t[:, tc * 512:(tc + 1) * 512], [ok])
            self.norm_epilogue2(*pend)
            for h, (dst, dk) in enumerate(dsts):
                P.dma('pool', dst, o_t[h * 64:(h + 1) * 64, :], [ok], [dk])

        def raw_block(c0, dsts):
            wt, wk = self.load_w(c0, 128)
            o_t, ok = outT.next()
            for tc in range(NC):
                z, zk = psz.next()
                self.fm(wt, wk, 0, 128, tc, z, zk)
                P.act(o_t[:, tc * 512:(tc + 1) * 512], z[:, 0:512], AF.Copy, [zk], [ok])
            for h, (dst, dk) in enumerate(dsts):
                P.dma('pool', dst, o_t[h * 64:(h + 1) * 64, :], [ok], [dk])

        for hb in range(2):
            normed_block(L_QA + hb * 128, g_qa, "g_qa",
                         [(self.fox_qT[2 * hb, 0:64, :], 'fox_qT'), (self.fox_qT[2 * hb + 1, 0:64, :], 'fox_qT')])
            normed_block(L_KA + hb * 128, g_ka, "g_ka",
                         [(self.fox_kT[2 * hb, 0:64, :], 'fox_kT'), (self.fox_kT[2 * hb + 1, 0:64, :], 'fox_kT')])
        for hb in range(2):
            normed_block(L_QC + hb * 128, g_qc, "g_qc",
                         [(self.nsa_qT[2 * hb, :, :], 'nsa_qT'), (self.nsa_qT[2 * hb + 1, :, :], 'nsa_qT')])
        raw_block(L_KCC, [(self.kcT, 'kcT'), (self.vcT, 'vcT')])
        normed_block(L_KSC, g_kc, "g_kc", [(self.kslcT, 'kslcT'), (self.kwinT, 'kwinT')])
        P.barrier()
        s.close()

    def phase_J_forget(self, psz):
        P, S, T, NC, l = self.P, self.S, self.T, self.NC, self.l
        pr = self.pr
        s2 = Scope(self.nc)
        negb = s2.sb("negb", [NH, 1], F32)
        P.dma('sp', negb[:], pr['b_forget'][l].rearrange("(p o) -> p o", o=1), [], ['negb'])
        P.v('dve', 'tensor_scalar', ['negb'], ['negb'], negb[:], negb[:], -1.0, None, ALU.mult)
        ones3 = s2.sb("ones3", [NH, 3, 512], BF16)
        P.v('pool', 'memset', [], ['ones3'], ones3[:], 1.0)
        wt, wk = self.load_w(L_FA, NH)
        fe = Rot(s2.sb, "fe", [NH, 512], F32, 2)
        fcs = Rot(s2.sb, "fcs", [NH, 512], F32, 2)
        fr = Rot(s2.sb, "fr", [NH, 512], F32, 2)
        fparts = Rot(s2.sb, "fparts", [NH, 3, 512], BF16, 2)
        fneg = Rot(s2.sb, "fneg", [NH, 3, 512], BF16, 2)
        prev = None
        for tc in range(NC):
            z, zk = psz.next()
            self.fm(wt, wk, 0, NH, tc, z, zk)
            e_t, ek = fe.next()
            c_t, ck = fcs.next()
            r_t, rk = fr.next()
            p_t, pk = fparts.next()
            n_t, nk = fneg.next()
            P.act(e_t[:], z[0:NH, 0:512], AF.Exp, [zk, 'negb'], [ek], scale=-1.0, bias=negb[:, 0:1])
            P.act(e_t[:], e_t[:], AF.Ln, [ek], [ek], bias=1.0)
            init = 0.0 if prev is None else prev[0][:, 511:512]
            rkeys = [ek, 'ones_f'] + ([] if prev is None else [prev[1]])
            P.v('dve', 'tensor_tensor_scan', rkeys, [ck], c_t[:], self.ones_f[0:NH, 0:1].to_broadcast([NH, 512]), e_t[:], init, ALU.mult, ALU.add)
            prev = (c_t, ck)
            P.v('dve', 'tensor_copy', [ck], [pk], p_t[:, 0, :], c_t[:])
            P.v('dve', 'tensor_tensor', [ck, pk], [rk], r_t[:], c_t[:], p_t[:, 0, :], ALU.subtract)
            P.v('dve', 'tensor_copy', [rk], [pk], p_t[:, 1, :], r_t[:])
            P.v('dve', 'tensor_tensor', [rk, pk], [rk], r_t[:], r_t[:], p_t[:, 1, :], ALU.subtract)
            P.v('dve', 'tensor_copy', [rk], [pk], p_t[:, 2, :], r_t[:])
            P.v('dve', 'tensor_scalar', [pk], [nk], n_t[:], p_t[:], -1.0, None, ALU.mult)
            sl = slice(tc * 512, (tc + 1) * 512)
            P.dma('pool', self.fox_kT[:, 64:67, sl], p_t[:], [pk], ['fox_kT'])
            P.dma('pool', self.fox_kT[:, 67:70, sl], ones3[:], ['ones3'], ['fox_kT'])
            P.dma('pool', self.fox_qT[:, 64:67, sl], ones3[:], ['ones3'], ['fox_qT'])
            P.dma('pool', self.fox_qT[:, 67:70, sl], n_t[:], [nk], ['fox_qT'])
        P.barrier()
        s2.close()

    def phase_J_tok(self, psz):
        P, S, T, NC, l = self.P, self.S, self.T, self.NC, self.l
        s2 = Scope(self.nc)
        vaug = Rot(s2.sb, "vaug", [128, NH, 65], BF16, 6)
        for i_, t_ in enumerate(vaug.t):
            P.v('pool', 'memset', [], [('vaug', i_)], t_[:], 1.0)
        wt, wk = self.load_w(L_VA, WB)
        for tt in range(T):
            z, zk = psz.next()
            self.tm(wt, wk, 0, WB, tt, z, zk)
            v_t, vk = vaug.next()
            P.v('dve', 'tensor_copy', [zk], [vk], v_t[:, :, 0:64], z[:, 0:WB].rearrange("p (h d) -> p h d", d=64))
            P.dma('pool', self.fox_v[tt * 128:(tt + 1) * 128, :, :], v_t[:], [vk], [P.uk('fox_v')])
        gsb = Rot(s2.sb, "gsb", [128, WB], F32, 8)
        for (c0, dst, dk) in ((L_GA, self.ga_s, 'ga_s'), (L_GC, self.gc_s, 'gc_s')):
            wt, wk = self.load_w(c0, WB)
            for tt in range(T):
                z, zk = psz.next()
                self.tm(wt, wk, 0, WB, tt, z, zk)
                g_t, gk = gsb.next()
                P.act(g_t[:], z[:, 0:WB], AF.Silu, [zk], [gk])
                P.dma('pool', dst[tt * 128:(tt + 1) * 128, :], g_t[:], [gk], [P.uk(dk)])
        vsl = s2.sb("vsl_all", [128, T, 65], BF16)
        vwn = s2.sb("vwn_all", [128, T, 65], BF16)
        P.v('pool', 'memset', [], ['vsl_all'], vsl[:], 1.0)
        P.v('pool', 'memset', [], ['vwn_all'], vwn[:], 1.0)
        gat = s2.sb("gat", [128, T, 12], F32)
        wt, wk = self.load_w(L_VSC, 140)
        for tt in range(T):
            z, zk = psz.next()
            self.tm(wt, wk, 0, 140, tt, z, zk)
            P.v('dve', 'tensor_copy', [zk], ['vsl_all'], vsl[:, tt, 0:64], z[:, 0:64])
            P.v('dve', 'tensor_copy', [zk], ['vwn_all'], vwn[:, tt, 0:64], z[:, 64:128])
            P.act(gat[:, tt, :], z[:, 128:140], AF.Sigmoid, [zk], ['gat'])
        P.dma('pool', self.vslc.rearrange("p (t c) -> p t c", c=65), vsl[:], ['vsl_all'], ['vslc'])
        P.dma('pool', self.vwin.rearrange("p (t c) -> p t c", c=65), vwn[:], ['vwn_all'], ['vwin'])
        P.dma('pool', self.gates.rearrange("p (t c) -> p t c", c=12), gat[:], ['gat'], ['gates'])
        P.barrier()
        s2.close()

    def phase_J_B(self, psz):
        P, S, NC, l = self.P, self.S, self.NC, self.l
        pr = self.pr
        s = Scope(self.nc)
        col = lambda v: v.rearrange("(p o) -> p o", o=1)
        prm = s.sb("b_prm", [128, 2, 8], F32)
        for cb in range(2):
            sl = slice(cb * 128, (cb + 1) * 128)
            for j in range(4):
                P.dma('sp', prm[:, cb, j:j + 1], col(pr['conv_w'][l, j, sl]), [], ['b_prm'])
            for j, nm in enumerate(('conv_b', 'b_rg_a', 'b_rg_x', 'lru_lambda')):
                P.dma('sp', prm[:, cb, 4 + j:5 + j], col(pr[nm][l, sl]), [], ['b_prm'])
        nsp8 = s.sb("nsp8", [128, 2, 1], F32)
        P.act(nsp8[:], prm[:, :, 7:8], AF.Exp, ['b_prm'], ['nsp8'], scale=-1.0)
        P.act(nsp8[:], nsp8[:], AF.Ln, ['nsp8'], ['nsp8'], bias=1.0)
        P.v('dve', 'tensor_scalar', ['nsp8'], ['nsp8'], nsp8[:], nsp8[:], -8.0, None, ALU.mult)
        wgf = s.sb("wgf", [128, 2, 2, 128], F32)
        P.v('pool', 'memset', [], ['wgf'], wgf[:], 0.0)
        for wi, nm in enumerate(('w_rg_a', 'w_rg_x')):
            for cb in range(2):
                for h in range(2):
                    P.dma('sp', wgf[h * 64:(h + 1) * 64, wi, cb, h * 64:(h + 1) * 64], pr[nm][l, 2 * cb + h], [], ['wgf'])
        wg = s.sb("wg", [128, 2, 2, 128], BF16)
        P.v('dve', 'tensor_copy', ['wgf'], ['wg'], wg[:], wgf[:])
        xbe = Rot(s.sb, "xbe", [128, 515], F32, 4)
        xc = Rot(s.sb, "xc", [128, 512], F32,3)
        xcb = Rot(s.sb, "xcb", [128, 512], BF16,2)
        rr = Rot(s.sb, "rr", [128, 512], F32,2)
        ig = Rot(s.sb, "ig", [128, 512], F32,3)
        aa = Rot(s.sb, "aa", [128, 512], F32,2)
        a2 = Rot(s.sb, "a2", [128, 512], F32,2)
        hh = Rot(s.sb, "hh", [128, 512], F32, 4)
        sg = Rot(s.sb, "sg", [128, 512], F32,2)
        wx = [s.sb("b_wx%d" % cb, [128, 8, 128], BF16) for cb in range(2)]
        wgt = [s.sb("b_wg%d" % cb, [128, 8, 128], BF16) for cb in range(2)]
        ysB = [s.sb("ysB%d" % cb, [128, S], BF16) for cb in range(2)]
        for cb in range(2):
            P.dma('sp', wx[cb][:], self.w_in_b[:, L_XB + cb * 128:L_XB + (cb + 1) * 128].rearrange("(kc p) n -> p kc n", p=128), ['w_in_b'], ['b_wx%d' % cb])
            P.dma('sp', wgt[cb][:], self.w_in_b[:, L_GB + cb * 128:L_GB + (cb + 1) * 128].rearrange("(kc p) n -> p kc n", p=128), ['w_in_b'], ['b_wg%d' % cb])
        prev_h = [None, None]
        prev_x = [None, None]
        for tc in range(NC):
            st = [dict(), dict()]
            for cb in range(2):
                d = st[cb]
                x_t, xk = xbe.next()
                z, zk = psz.next()
                self.fm(wx[cb], 'b_wx%d' % cb, 0, 128, tc, z, zk)
                P.act(x_t[:, 3:515], z[:, 0:512], AF.Copy, [zk], [xk])
                if prev_x[cb] is None:
                    P.v('pool', 'memset', [], [xk], x_t[:, 0:3], 0.0)
                else:
                    P.v('pool', 'tensor_copy', [prev_x[cb][1]], [xk], x_t[:, 0:3], prev_x[cb][0][:, 512:515])
                prev_x[cb] = (x_t, xk)
                c_t, ck = xc.next()
                P.v('dve', 'tensor_scalar', [xk, 'b_prm'], [ck], c_t[:], x_t[:, 3:515], prm[:, cb, 3:4], prm[:, cb, 4:5], ALU.mult, ALU.add)
                for j in range(3):
                    P.v('dve', 'scalar_tensor_tensor', [xk, 'b_prm', ck], [ck], c_t[:], x_t[:, j:j + 512], prm[:, cb, j:j + 1], c_t[:], ALU.mult, ALU.add)
                cb_t, cbk = xcb.next()
                P.v('dve', 'tensor_copy', [ck], [cbk], cb_t[:], c_t[:])
                d.update(c=(c_t, ck), cbt=(cb_t, cbk))
            for cb in range(2):
                d = st[cb]
                cb_t, cbk = d['cbt']
                za, zak = psz.next()
                P.mm(za[:, 0:512], wg[:, 0, cb, :], cb_t[:], True, True, ['wg', cbk], [zak])
                r_t, rk = rr.next()
                P.act(r_t[:], za[:, 0:512], AF.Sigmoid, [zak, 'b_prm'], [rk], bias=prm[:, cb, 5:6])
                zx, zxk = psz.next()
                P.mm(zx[:, 0:512], wg[:, 1, cb, :], cb_t[:], True, True, ['wg', cbk], [zxk])
                i_t, ik = ig.next()
                P.act(i_t[:], zx[:, 0:512], AF.Sigmoid, [zxk, 'b_prm'], [ik], bias=prm[:, cb, 6:7])
                d.update(r=(r_t, rk), i=(i_t, ik))
            for cb in range(2):
                d = st[cb]
                r_t, rk = d['r']
                a_t, ak = aa.next()
                P.act(a_t[:], r_t[:], AF.Exp, [rk, 'nsp8'], [ak], scale=nsp8[:, cb, 0:1])
                q_t, qk = a2.next()
                P.v('pool', 'tensor_tensor', [ak], [qk], q_t[:], a_t[:], a_t[:], ALU.mult)
                d.update(a=(a_t, ak), q=(q_t, qk))
            for cb in range(2):
                q_t, qk = st[cb]['q']
                P.act(q_t[:], q_t[:], AF.Sqrt, [qk], [qk], scale=-1.0, bias=1.0)
            for cb in range(2):
                zg, zgk = psz.next()
                self.fm(wgt[cb], 'b_wg%d' % cb, 0, 128, tc, zg, zgk)
                s_t, sk = sg.next()
                P.act(s_t[:], zg[:, 0:512], AF.Silu, [zgk], [sk])
                st[cb]['s'] = (s_t, sk)
            for cb in range(2):
                d = st[cb]
                i_t, ik = d['i']
                c_t, ck = d['c']
                q_t, qk = d['q']
                a_t, ak = d['a']
                s_t, sk = d['s']
                P.v('dve', 'tensor_tensor', [ik, ck], [ik], i_t[:], i_t[:], c_t[:], ALU.mult)
                P.v('dve', 'tensor_tensor', [ik, qk], [ik], i_t[:], i_t[:], q_t[:], ALU.mult)
                h_t, hk = hh.next()
                init = 0.0 if prev_h[cb] is None else prev_h[cb][0][:, 511:512]
                rkeys = [ak, ik] + ([] if prev_h[cb] is None else [prev_h[cb][1]])
                P.v('dve', 'tensor_tensor_scan', rkeys, [hk], h_t[:], a_t[:], i_t[:], init, ALU.mult, ALU.add)
                prev_h[cb] = (h_t, hk)
                P.v('dve', 'tensor_tensor', [hk, sk], ['ysB%d' % cb], ysB[cb][:, tc * 512:(tc + 1) * 512], h_t[:], s_t[:], ALU.mult)
        for cb in range(2):
            P.dma('pool', self.ysT[WB + cb * 128:WB + (cb + 1) * 128, :], ysB[cb][:], ['ysB%d' % cb], [('ysT', 1)])
        self.allgather(self.ysT[WB:2 * WB, :], self.ysT_all[W:2 * W, :], [('ysT', 1)], ['ysT_all'])
        P.barrier()
        s.close()

    def phase_J_D(self, psz):
        P, S, NC, l = self.P, self.S, self.NC, self.l
        pr = self.pr
        s = Scope(self.nc)
        wsf = s.sb("wsf", [128, 4, 128], F32)
        P.dma('sp', wsf[:], pr['w_spatial'][l].rearrange("g t s -> t g s"), [], ['wsf'])
        wsb = s.sb("wsb", [128, 4, 128], BF16)
        P.v('dve', 'tensor_tensor', ['wsf', 'c_tril'], ['wsb'], wsb[:], wsf[:], self.c['tril'][:].unsqueeze(1).to_broadcast([128, 4, 128]), ALU.mult)
        wsT = s.sb("wsT", [128, 4, 128], BF16)
        pst = s.ps("psT_d", [128, 4, 128], BF16)
        for g in range(4):
            P.tr(pst[:, g, :], wsb[:, g, :], self.c['ident'][:], ['wsb', 'c_ident'], ['psT_d'])
        P.v('dve', 'tensor_copy', ['psT_d'], ['wsT'], wsT[:], pst[:])
        bsb = s.sb("bsb", [128, 2, 128], F32)
        for gp in range(2):
            for h in range(2):
                P.dma('sp', bsb[h * 64:(h + 1) * 64, gp, :], pr['b_spatial'][l, 2 * gp + h].partition_broadcast(64), [], ['bsb'])
        lng = s.sb("lng", [128, W], F32)
        P.dma('sp', lng[:], pr['ln_v_g'][l].partition_broadcast(128), [], ['lng'])
        wtu, wku = self.load_w(L_UD, WB)
        wtv, wkv = self.load_w(L_VD, 512)
        wtg, wkg = self.load_w(L_GD, WB)
        ug = Rot(s.sb, "ug", [128, 2, 512], F32, 2)
        usb = Rot(s.sb, "usb", [128, 512], F32, 2)
        gsb = Rot(s.sb, "gsb_d", [128, 512], F32, 2)
        junk = s.sb("junk_d", [128, 512], BF16)
        vln = Rot(s.sb, "vln", [128, 512], BF16, 2)
        psm = Rot(s.ps, "psm", [128, 2, 128], F32, 2)
        t1 = Rot(s.sb, "t1_d", [128, 2, 128], F32, 2)
        ysd = Rot(s.sb, "ysd", [128, 2, 512], BF16, 2)
        gv = Rot(s.sb, "gv4", [128, 512], F32, 5)
        st4 = Rot(s.sb, "st4", [128, 4, 4], F32, 2)
        for tc in range(NC):
            u_t, uk = ug.next()
            bts = []
            for blk in range(2):
                z2, z2k = psz.next()
                self.fm(wtg, wkg, blk * 128, 128, tc, z2, z2k)
                b_t, bk = gsb.next()
                P.act(b_t[:], z2[:, 0:512], AF.Silu, [z2k], [bk])
                bts.append((b_t, bk))
            for blk in range(2):
                z, zk = psz.next()
                self.fm(wtu, wku, blk * 128, 128, tc, z, zk)
                a_t, ak = usb.next()
                P.act(a_t[:], z[:, 0:512], AF.Gelu, [zk], [ak])
                b_t, bk = bts[blk]
                P.v('pool', 'tensor_tensor', [ak, bk], [uk], u_t[:, blk, :], a_t[:], b_t[:], ALU.mult)
            s_t, sk = st4.next()
            gts_ = []
            for q in range(4):
                tt = tc * 4 + q
                z, zk = psz.next()
                self.tm(wtv, wkv, 0, 512, tt, z, zk)
                g_t, gk = gv.next()
                P.act(g_t[:], z[:, 0:512], AF.Gelu, [zk], [gk, sk], accum_out=s_t[:, q, 0:1])
                gts_.append((g_t, gk))
            P.v('dve', 'tensor_scalar', [sk], [sk], s_t[:, :, 1:2], s_t[:, :, 0:1], -1.0 / W, None, ALU.mult)
            for q in range(4):
                g_t, gk = gts_[q]
                P.act(junk[:], g_t[:], AF.Square, [gk, sk], ['junk_d', sk], bias=s_t[:, q, 1:2], accum_out=s_t[:, q, 2:3])
            P.act(s_t[:, :, 3:4], s_t[:, :, 2:3], AF.Ln, [sk], [sk], scale=1.0 / W, bias=EPS)
            P.act(s_t[:, :, 3:4], s_t[:, :, 3:4], AF.Exp, [sk], [sk], scale=-0.5)
            y_t, yk = ysd.next()
            for q in range(4):
                g_t, gk = gts_[q]
                P.v('dve', 'tensor_scalar', [gk, sk], [gk], g_t[:], g_t[:], s_t[:, q, 1:2], s_t[:, q, 3:4], ALU.add, ALU.mult)
                v_t, vk = vln.next()
                P.v('dve', 'tensor_tensor', [gk, 'lng'], [vk], v_t[:], g_t[:], lng[:], ALU.mult)
                m_t, mk = psm.next()
                for g in range(4):
                    P.mm(m_t[(g % 2) * 64:(g % 2) * 64 + 64, g // 2, :], v_t[:, g * 64:(g + 1) * 64], wsT[:, g, :], True, True, [vk, 'wsT'], [mk])
                a_t, ak = t1.next()
                P.v('dve', 'tensor_tensor', [mk, 'bsb'], [ak], a_t[:], m_t[:], bsb[:], ALU.add)
                P.v('dve', 'tensor_tensor', [ak, uk], [yk], y_t[:, :, q * 128:(q + 1) * 128], a_t[:], u_t[:, :, q * 128:(q + 1) * 128], ALU.mult)
            P.dma('pool', self.ysT[3 * WB:4 * WB, tc * 512:(tc + 1) * 512].rearrange("(b p) n -> p b n", p=128), y_t[:], [yk], [('ysT', 3)])
        self.allgather(self.ysT[3 * WB:4 * WB, :], self.ysT_all[3 * W:4 * W, :], [('ysT', 3)], ['ysT_all'])
        P.barrier()
        s.close()

    def attn_chunk(self, qT, qk, ktiles, acc, acck, ncols, pss, pT):
        P = self.P
        n = len(ktiles)
        st = [None] * n
        LOOK = len(pss.t) - 1

        def emit_S(i):
            kt = ktiles[i]
            s_t, sk = pss.next()
            q0 = kt['q0']
            q1 = kt.get('q1', 512)
            masks = kt['masks']
            P.mm(s_t[:, q0:q1], kt['lhsT'], qT[:, q0:q1], True, len(masks) == 0, kt['keys'] + [qk], [sk])
            for mi, (ml, mr, c0, c1, mkeys) in enumerate(masks):
                P.mm(s_t[:, c0:c1], ml, mr, False, mi == len(masks) - 1, mkeys, [sk])
            st[i] = (s_t, sk)

        for i in range(min(LOOK, n)):
            emit_S(i)
        for i in range(n):
            if i + LOOK < n:
                emit_S(i + LOOK)
            kt = ktiles[i]
            s_t, sk = st[i]
            q0 = kt['q0']
            q1 = kt.get('q1', 512)
            p_t, pk = pT.next()
            P.act(p_t[:, q0:q1], s_t[:, q0:q1], AF.Exp, [sk], [pk])
            for qq in kt['qts']:
                P.mm(acc[:, qq, 0:ncols], p_t[:, qq * 128:(qq + 1) * 128], kt['v'], kt['first'][qq], kt['last'][qq], [pk] + kt['vkeys'], [acck])

    def phase_A(self):
        P, S, T, NC = self.P, self.S, self.T, self.NC
        s = Scope(self.nc)
        kT = Rot(s.sb, "a_kT", [70, S], BF16, 2)
        qT = Rot(s.sb, "a_qT", [70, S], BF16, 2)
        va = Rot(s.sb, "a_v", [128, T, 65], BF16, 2)
        pss = Rot(s.ps, "a_ps", [128, 512], F32, 4)
        accs = Rot(s.ps, "a_acc", [128, 4, 512], F32, 1)
        pT = Rot(s.sb, "a_pT", [128, 512], BF16, 3)
        rden = Rot(s.sb, "a_rd", [128, 4, 1], F32, 2)
        osb = Rot(s.sb, "a_o", [128, 4, 64], F32, 2)
        ident, causal = self.c['ident'], self.c['causal']
        for h in range(NH):
            k_t, kk = kT.next()
            q_t, qk = qT.next()
            v_t, vk = va.next()
            P.dma('sp', k_t[:], self.fox_kT[h], ['fox_kT'], [kk])
            P.dma('sp', q_t[:], self.fox_qT[h], ['fox_qT'], [qk])
            P.dma('sp', v_t[:], self.fox_v[:, h, :].rearrange("(t p) c -> p t c", p=128), ['fox_v'], [vk])
            for tc in range(NC):
                acc, acck = accs.next()
                ktiles = []
                for kj in range(4 * tc + 4):
                    diag = kj >= 4 * tc
                    qlo = max(kj, 4 * tc) - 4 * tc
                    masks = []
                    if diag:
                        masks.append((ident[:], causal[:], qlo * 128, qlo * 128 + 128, ['c_ident', 'c_causal']))
                    qts = list(range(qlo, 4))
                    ktiles.append(dict(lhsT=k_t[:, kj * 128:(kj + 1) * 128], keys=[kk], v=v_t[:, kj, :], vkeys=[vk],
                                       q0=qlo * 128, masks=masks, qts=qts,
                                       first={qq: kj == 0 for qq in qts}, last={qq: kj == 4 * tc + qq for qq in qts}))
                self.attn_chunk(q_t[:, tc * 512:(tc + 1) * 512], qk, ktiles, acc, acck, 65, pss, pT)
                r_t, rk = rden.next()
                o_t, ok = osb.next()
                P.v('dve', 'reciprocal', [acck], [rk], r_t[:], acc[:, :, 64:65])
                P.v('dve', 'tensor_tensor', [acck, rk], [ok], o_t[:], acc[:, :, 0:64], r_t[:].to_broadcast([128, 4, 64]), ALU.mult)
                P.dma('pool', self.y_a[tc * 512:(tc + 1) * 512, h * 64:(h + 1) * 64].rearrange("(q p) d -> p q d", p=128), o_t[:], [ok], [P.uk('y_a')])
        P.barrier()
        s.close()

    def phase_C(self):
        P, S, T, NC, l = self.P, self.S, self.T, self.NC, self.l
        pr = self.pr
        c = self.c
        n_cmp = (S - 32) // 16 + 1
        NCT = (n_cmp + 127) // 128
        s = Scope(self.nc)
        self.alloc_consts(s, ['E', 'M0', 'cand', 'tbl2'])
        self.stage = Rot(s.sb, "stage_c", [128, 2048], F32, 2)
        self.load_consts(['E', 'M0', 'cand', 'tbl2'])
        w1 = {}
        w2 = {}
        for nm in ('k', 'v'):
            w1[nm] = s.sb("w1_" + nm, [64, 32, 128], BF16)
            src = pr['cmp_%s_w1' % nm][l].rearrange("(i d) h -> d i h", d=64)
            for hf in range(2):
                st, sk = self.stage.next()
                P.dma('sp', st[0:64, :].rearrange("p (i h) -> p i h", h=128), src[:, hf * 16:(hf + 1) * 16, :], [], [sk])
                P.v('dve', 'tensor_copy', [sk], ['w1_' + nm], w1[nm][:, hf * 16:(hf + 1) * 16, :], st[0:64, :].rearrange("p (i h) -> p i h", h=128))
            st, sk = self.stage.next()
            P.dma('sp', st[:, 0:64], pr['cmp_%s_w2' % nm][l], [], [sk])
            w2[nm] = s.sb("w2_" + nm, [128, 64], BF16)
            P.v('dve', 'tensor_copy', [sk], ['w2_' + nm], w2[nm][:], st[:, 0:64])
        posf = s.sb("posf", [64, 32], F32)
        P.dma('sp', posf[:], pr['cmp_pos'][l].rearrange("i d -> d i"), [], ['posf'], allow_slow_non_contiguous=True)
        posT = s.sb("posT", [64, 32], BF16)
        P.v('dve', 'tensor_copy', ['posf'], ['posT'], posT[:], posf[:])
        g_kc = self.load_col(s, "g_kc2", pr['kn_c'][l])
        pb = s.sb("pb", [128, 2], F32)
        pss = Rot(s.ps, "c_ps", [128, 512], F32, 3)
        accs = Rot(s.ps, "c_acc", [128, 4, 512], F32, 1)
        for wi, nm in enumerate(('k', 'v')):
            z, zk = pss.next()
            for i in range(32):
                P.mm(z[:, 0:1], w1[nm][:, i, :], posT[:, i:i + 1], i == 0, i == 31, ['w1_' + nm, 'posT'], [zk])
            P.v('dve', 'tensor_copy', [zk], ['pb'], pb[:, wi:wi + 1], z[:, 0:1])
        xc_in = Rot(s.sb, "xc_in", [64, S], BF16, 1)
        hid = Rot(s.sb, "hid", [128, 256], BF16, 2)
        kcn = s.sb("kcn", [64, 256], BF16)
        sqc = s.sb("sqc", [64, 256], BF16)
        rsc = s.sb("rsc", [64, 256], F32)
        vca = s.sb("vca", [128, 2, 128], BF16)
        qrot = Rot(s.sb, "c_qT", [128, 4, 512], BF16, 2)
        kwin = s.sb("c_kwin", [64, S], BF16)
        vsl = s.sb("c_vsl", [128, T, 65], BF16)
        vwn = s.sb("c_vwn", [128, T, 65], BF16)
        gts = s.sb("c_gts", [128, T, 12], F32)
        P.dma('sp', gts[:], self.gates.rearrange("p (t c) -> p t c", c=12), ['gates'], ['c_gts'])
        pT = Rot(s.sb, "c_pT", [128, 512], BF16, 3)
        imp = s.sb("c_imp", [128, 4, 64], F32)
        tmp = Rot(s.sb, "c_tmp", [128, 4, 64], F32, 2)
        rden = Rot(s.sb, "c_rd", [128, 4, 1], F32, 3)
        coef = Rot(s.sb, "c_coef", [128, 4, 1], F32, 3)
        sc = Rot(s.sb, "c_sc", [128, 64], F32, 2)
        sc2 = Rot(s.sb, "c_sc2", [128, 64], F32, 2)
        m8 = Rot(s.sb, "c_m8", [128, 8], F32, 2)
        nsel = Rot(s.sb, "c_nsel", [128, 64], BF16, 4)
        pstr = Rot(s.ps, "c_pstr", [128, 128], BF16, 1)
        ybuf = Rot(s.sb, "c_ybuf", [128, 4, 256], F32, 2)
        ident, causal, window, E, M0 = c['ident'], c['causal'], c['window'], c['E'], c['M0']

        for g in range(1):
            gs = slice(0, 64)
            P.v('pool', 'memset', [], ['vca'], vca[:], 0.0)
            for wi, (nm, src, skey) in enumerate((('k', self.kcT, 'kcT'), ('v', self.vcT, 'vcT'))):
                x_t, xk = xc_in.next()
                P.dma('sp', x_t[:], src[gs, :], [skey], [xk])
                z, zk = pss.next()
                for i in range(32):
                    P.mm(z[:, 0:n_cmp], w1[nm][:, i, :], x_t[:, i:i + 16 * (n_cmp - 1) + 1:16], i == 0, i == 31, ['w1_' + nm, xk], [zk])
                h_t, hk = hid.next()
                P.v('pool', 'memset', [], [hk], h_t[:], 0.0)
                P.act(h_t[:, 0:n_cmp], z[:, 0:n_cmp], AF.Silu, [zk, 'pb'], [hk], bias=pb[:, wi:wi + 1])
                if nm == 'k':
                    z2, z2k = pss.next()
                    P.mm(z2[0:64, 0:256], w2['k'][:], h_t[:], True, True, ['w2_k', hk], [z2k])
                    P.act(sqc[:], z2[0:64, 0:256], AF.Square, [z2k], ['sqc'])
                    z3, z3k = pss.next()
                    P.mm(z3[0:64, 0:256], c['bones'][0:64, 0:64], sqc[:], True, True, ['sqc', 'c_bones'], [z3k])
                    P.act(rsc[:], z3[0:64, 0:256], AF.Ln, [z3k], ['rsc'], scale=1.0 / 64, bias=EPS)
                    P.act(rsc[:], rsc[:], AF.Exp, ['rsc'], ['rsc'], scale=-0.5)
                    P.v('dve', 'scalar_tensor_tensor', [z2k, 'g_kc2', 'rsc'], ['kcn'], kcn[:], z2[0:64, 0:256], g_kc[0:64, 0:1], rsc[:], ALU.mult, ALU.mult)
                else:
                    for ct in range(NCT):
                        z2, z2k = pss.next()
                        P.mm(z2[:, 0:64], h_t[:, ct * 128:(ct + 1) * 128], w2['v'][:], True, True, [hk, 'w2_v'], [z2k])
                        P.v('dve', 'tensor_copy', [z2k], ['vca'], vca[:, ct, 0:64], z2[:, 0:64])
            for ct in range(2):
                P.v('pool', 'tensor_copy', ['c_overlap'], ['vca'], vca[:, ct, 64:128], c['overlap'][:, ct * 64:(ct + 1) * 64])
            P.dma('sp', E[0:64, :], self.kslcT[gs, :], ['kslcT'], ['c_E'])
            P.dma('sp', kwin[:], self.kwinT[gs, :], ['kwinT'], ['c_kwin'])
            P.dma('sp', vsl[:], self.vslc.rearrange("p (t c) -> p t c", c=65), ['vslc'], ['c_vsl'])
            P.dma('sp', vwn[:], self.vwin.rearrange("p (t c) -> p t c", c=65), ['vwin'], ['c_vwn'])
            for tc in range(NC):
                y_t, yk = ybuf.next()
                tsl = slice(tc * 512, (tc + 1) * 512)
                qTs, qTk = qrot.next()
                P.dma('sp', qTs[0:64], self.nsa_qT[4 * g:4 * g + 4, :, tsl].rearrange("h d n -> d h n"), ['nsa_qT'], [qTk])
                for r in range(4):
                    h = 4 * g + r
                    acc, acck = accs.next()
                    ktiles = []
                    cts = [ct for ct in range(NCT) if ct * 2048 + 31 < (tc + 1) * 512]
                    for ci, ct in enumerate(cts):
                        m_off = tc * 512 - ct * 2048
                        if m_off >= 0:
                            masks = [(ident[:], M0[:, m_off:m_off + 512], 0, 512, ['c_ident', 'c_M0'])]
                        else:
                            masks = []
                        qts = [0, 1, 2, 3]
                        ktiles.append(dict(lhsT=kcn[:, ct * 128:(ct + 1) * 128], keys=['kcn'], v=vca[:, ct, :], vkeys=['vca'],
                                           q0=0, masks=masks, qts=qts,
                                           first={qq: ci == 0 for qq in qts}, last={qq: ci == len(cts) - 1 for qq in qts}))
                    self.attn_chunk(qTs[0:64, r, :], qTk, ktiles, acc, acck, 128, pss, pT)
                    r_t, rk = rden.next()
                    P.v('dve', 'tensor_reduce', [acck], [rk], r_t[:].rearrange("p q o -> p (q o)"), acc[:, :, 64:128], AX.X, ALU.add)
                    P.v('dve', 'tensor_scalar', [rk], [rk], r_t[:], r_t[:], 1e-20, None, ALU.add)
                    P.v('dve', 'reciprocal', [rk], [rk], r_t[:], r_t[:])
                    c_t, ck = coef.next()
                    P.v('dve', 'tensor_tensor', [rk, 'c_gts'], [ck], c_t[:], r_t[:], gts[:, tc * 4:tc * 4 + 4, r:r + 1], ALU.mult)
                    P.v('dve', 'tensor_tensor', [acck, ck], [yk], y_t[:, :, r * 64:(r + 1) * 64], acc[:, :, 0:64], c_t[:].to_broadcast([128, 4, 64]), ALU.mult)
                    if r == 0:
                        P.v('dve', 'tensor_tensor', [acck, rk], ['c_imp'], imp[:], acc[:, :, 64:128], r_t[:].to_broadcast([128, 4, 64]), ALU.mult)
                    else:
                        t_t, tk = tmp.next()
                        P.v('dve', 'tensor_tensor', [acck, rk], [tk], t_t[:], acc[:, :, 64:128], r_t[:].to_broadcast([128, 4, 64]), ALU.mult)
                        P.v('dve', 'tensor_tensor', ['c_imp', tk], ['c_imp'], imp[:], imp[:], t_t[:], ALU.add)
                nsel_list = []
                for qq in range(4):
                    tt = tc * 4 + qq
                    s_t, sk = sc.next()
                    P.v('dve', 'tensor_tensor', ['c_imp', 'c_cand'], [sk], s_t[:], imp[:, qq, :], c['cand'][:, tt * 64:(tt + 1) * 64], ALU.mult)
                    P.v('dve', 'tensor_tensor', [sk, 'c_tbl2'], [sk], s_t[:], s_t[:], c['tbl2'][:, tt * 64:(tt + 1) * 64], ALU.add)
                    m_t, mk = m8.next()
                    s2_t, s2k = sc2.next()
                    P.v('dve', 'max', [sk], [mk], m_t[:], s_t[:])
                    P.v('dve', 'match_replace', [mk, sk], [s2k], s2_t[:], m_t[:], s_t[:], -2.0)
                    P.v('dve', 'max', [s2k], [mk], m_t[:], s2_t[:])
                    n_t, nk = nsel.next()
                    P.v('dve', 'tensor_scalar', [sk, mk], [s2k], s2_t[:], s_t[:], m_t[:, 7:8], -1.0, ALU.is_ge, ALU.add)
                    P.v('dve', 'tensor_scalar', [s2k], [nk], n_t[:], s2_t[:], BIG, None, ALU.mult)
                    nsel_list.append((n_t, nk, qq))
                for r in range(4):
                    h = 4 * g + r
                    acc, acck = accs.next()
                    ktiles = []
                    for kj in range(max(0, 4 * tc - 4), 4 * tc + 4):
                        qa = max(kj, 4 * tc) - 4 * tc
                        qb = min(kj + 4, 4 * tc + 3) - 4 * tc
                        masks = []
                        if kj >= 4 * tc:
                            masks.append((ident[:], causal[:], qa * 128, qa * 128 + 128, ['c_ident', 'c_causal']))
                        if kj + 4 <= 4 * tc + 3:
                            masks.append((ident[:], window[:], qb * 128, qb * 128 + 128, ['c_ident', 'c_window']))
                        qts = list(range(qa, qb + 1))
                        ktiles.append(dict(lhsT=kwin[:, kj * 128:(kj + 1) * 128], keys=['c_kwin'], v=vwn[:, kj, :], vkeys=['c_vwn'],
                                           q0=qa * 128, q1=(qb + 1) * 128, masks=masks, qts=qts,
                                           first={qq: kj == max(0, 4 * tc + qq - 4) for qq in qts},
                                           last={qq: kj == 4 * tc + qq for qq in qts}))
                    self.attn_chunk(qTs[0:64, r, :], qTk, ktiles, acc, acck, 65, pss, pT)
                    self.nsa_accum(acc, acck, y_t, yk, r, gts, tc, 8 + r, rden, coef, tmp)
                for (n_t, nk, qq) in nsel_list:
                    p_t, pk = pstr.next()
                    P.tr(p_t[64:128, :], n_t[:], ident[:], [nk, 'c_ident'], [pk])
                    P.v('dve', 'tensor_copy', [pk], [qTk], qTs[64:128, :, qq * 128:(qq + 1) * 128], p_t[64:128, :].unsqueeze(1).to_broadcast([64, 4, 128]))
                for r in range(4):
                    h = 4 * g + r
                    acc, acck = accs.next()
                    ktiles = []
                    for kj in range(4 * tc + 4):
                        diag = kj >= 4 * tc
                        qlo = max(kj, 4 * tc) - 4 * tc
                        masks = []
                        if diag:
                            masks.append((ident[:], causal[:], qlo * 128, qlo * 128 + 128, ['c_ident', 'c_causal']))
                        qts = list(range(qlo, 4))
                        ktiles.append(dict(lhsT=E[:, kj * 128:(kj + 1) * 128], keys=['c_E'], v=vsl[:, kj, :], vkeys=['c_vsl'],
                                           q0=qlo * 128, masks=masks, qts=qts,
                                           first={qq: kj == 0 for qq in qts}, last={qq: kj == 4 * tc + qq for qq in qts}))
                    self.attn_chunk(qTs[:, r, :], qTk, ktiles, acc, acck, 65, pss, pT)
                    self.nsa_accum(acc, acck, y_t, yk, r, gts, tc, 4 + r, rden, coef, tmp)
                P.dma('pool', self.y_c[tsl, :].rearrange("(q p) d -> p q d", p=128), y_t[:], [yk], [P.uk('y_c')])
        P.barrier()
        s.close()

    def nsa_accum(self, acc, acck, y_t, yk, r, gts, tc, gcol, rden, coef, tmp):
        P = self.P
        r_t, rk = rden.next()
        c_t, ck = coef.next()
        t_t, tk = tmp.next()
        P.v('dve', 'reciprocal', [acck], [rk], r_t[:], acc[:, :, 64:65])
        P.v('dve', 'tensor_tensor', [rk, 'c_gts'], [ck], c_t[:], r_t[:], gts[:, tc * 4:tc * 4 + 4, gcol:gcol + 1], ALU.mult)
        P.v('dve', 'tensor_tensor', [acck, ck], [tk], t_t[:], acc[:, :, 0:64], c_t[:].to_broadcast([128, 4, 64]), ALU.mult)
        P.v('dve', 'tensor_tensor', [yk, tk], [yk], y_t[:, :, r * 64:(r + 1) * 64], y_t[:, :, r * 64:(r + 1) * 64], t_t[:], ALU.add)

    def phase_G(self):
        P, S, T, NC = self.P, self.S, self.T, self.NC
        s = Scope(self.nc)
        yt = Rot(s.sb, "g_y", [128, WB], F32, 3)
        gt = Rot(s.sb, "g_g", [128, WB], F32, 3)
        yb = Rot(s.sb, "g_yb", [128, WB], BF16, 2)
        pst = Rot(s.ps, "g_ps", [128, 2, 128], BF16, 2)
        ob = Rot(s.sb, "g_ob", [128, 2, 512], BF16, 2)
        for (ysrc, yk0, gsrc, gk0, row0) in ((self.y_a, 'y_a', self.ga_s, 'ga_s', 0), (self.y_c, 'y_c', self.gc_s, 'gc_s', 2 * WB)):
            for tc in range(NC):
                o_t, ok = ob.next()
                for q in range(4):
                    tt = tc * 4 + q
                    y_t, yk = yt.next()
                    g_t, gk = gt.next()
                    b_t, bk = yb.next()
                    p_t, pk = pst.next()
                    P.dma('sp', y_t[:], ysrc[tt * 128:(tt + 1) * 128, :], [yk0], [yk])
                    P.dma('sp', g_t[:], gsrc[tt * 128:(tt + 1) * 128, :], [gk0], [gk])
                    P.v('dve', 'tensor_tensor', [yk, gk], [bk], b_t[:], y_t[:], g_t[:], ALU.mult)
                    for blk in range(2):
                        P.tr(p_t[:, blk, :], b_t[:, blk * 128:(blk + 1) * 128], self.c['ident'][:], [bk, 'c_ident'], [pk])
                    P.act(o_t[:, :, q * 128:(q + 1) * 128], p_t[:], AF.Copy, [pk], [ok])
                P.dma('pool', self.ysT[row0:row0 + WB, tc * 512:(tc + 1) * 512].rearrange("(b p) n -> p b n", p=128), o_t[:], [ok], [('ysT', row0 // WB)])
            n = row0 // WB
            self.allgather(self.ysT[n * WB:(n + 1) * WB, :], self.ysT_all[n * W:(n + 1) * W, :], [('ysT', n)], ['ysT_all'])
        P.barrier()
        s.close()

    def allgather(self, src, dst, r, w):
        self.P.add('pool', lambda e: e.collective_compute("AllGather", ALU.bypass, replica_groups=RG, ins=[src], outs=[dst]), r, w, cc=True)

    def phase_M(self, xin, xout):
        P, S, T, NC, l, L = self.P, self.S, self.T, self.NC, self.l, self.L
        s = Scope(self.nc)
        ys = Rot(s.sb, "m_ys", [128, 16, 512], BF16, 2)
        wmg = Rot(s.sb, "m_wmg", [128, 8, 4, 128], BF16, 2)
        wbr = Rot(s.sb, "m_wbr", [128, 16, 128], BF16, 2)
        psg = Rot(s.ps, "m_psg", [128, 512], F32, 2)
        psp = Rot(s.ps, "m_psp", [128, 512], F32, 2)
        sig = Rot(s.sb, "m_sig", [128, 512], F32, 2)
        macc = Rot(s.sb, "m_acc", [128, 512], F32, 2)
        mtmp = Rot(s.sb, "m_tmp", [128, 512], F32, 2)
        mT = Rot(s.sb, "m_mT", [128, 4, 512], BF16, 2)
        for tc in range(NC):
            tsl = slice(tc * 512, (tc + 1) * 512)
            y_t, yk = ys.next()
            P.dma('sp', y_t[:], self.ysT_all[:, tsl].rearrange("(c p) n -> p c n", p=128), ['ysT_all'], [yk])
            m_t, mk = mT.next()
            for dc in range(4):
                wm_t, wmk = wmg.next()
                wb_t, wbk = wbr.next()
                for n in range(4):
                    c0 = L_MG + n * W + dc * 128
                    P.dma('sp', wm_t[:, :, n, :], self.w_in_b[:, c0:c0 + 128].rearrange("(kc p) d -> p kc d", p=128), ['w_in_b'], [wmk])
                P.dma('sp', wb_t[:], self.w_br_b[:, dc * 128:(dc + 1) * 128].rearrange("(c p) d -> p c d", p=128), ['w_br_b'], [wbk])
                a_t, ak = macc.next()
                for n in range(4):
                    g_ps, gk = psg.next()
                    for kc in range(8):
                        P.mm(g_ps[:, 0:512], wm_t[:, kc, n, :], self.xnT[:, kc, tsl], kc == 0, kc == 7, [wmk, 'xnT'], [gk])
                    p_ps, pk = psp.next()
                    chunks = [n * 4 + k for k in range(4)]
                    for ci, c in enumerate(chunks):
                        P.mm(p_ps[:, 0:512], wb_t[:, c, :], y_t[:, c, :], ci == 0, ci == 3, [wbk, yk], [pk])
                    s_t, sk = sig.next()
                    P.act(s_t[:], g_ps[:, 0:512], AF.Sigmoid, [gk], [sk])
                    if n == 0:
                        P.v('dve', 'tensor_tensor', [pk, sk], [ak], a_t[:], p_ps[:, 0:512], s_t[:], ALU.mult)
                    else:
                        t_t, tk = mtmp.next()
                        P.v('dve', 'tensor_tensor', [pk, sk], [tk], t_t[:], p_ps[:, 0:512], s_t[:], ALU.mult)
                        if n < 3:
                            P.v('pool', 'tensor_tensor', [ak, tk], [ak], a_t[:], a_t[:], t_t[:], ALU.add)
                        else:
                            P.v('pool', 'tensor_tensor', [ak, tk], [mk], m_t[:, dc, :], a_t[:], t_t[:], ALU.add)
            P.dma('pool', self.mT[:, tsl].rearrange("(c p) n -> p c n", p=128), m_t[:], [mk], ['mT'])
        for j in range(2):
            self.allgather(self.mT[j * WB:(j + 1) * WB, :], self.mT_all[j * W:(j + 1) * W, :], ['mT'], ['mT_all'])
        P.barrier()
        s.close()
        s = Scope(self.nc)
        wo = s.sb("m_wo", [128, 8, W], BF16)
        P.dma('sp', wo[:], self.w_out_b.rearrange("(kc p) n -> p kc n", p=128), ['w_out_b'], ['m_wo'])
        mA = Rot(s.sb, "m_mA", [128, 8, 512], BF16, 2)
        pso = Rot(s.ps, "m_pso", [128, 512], F32, 2)
        xt = Rot(s.sb, "m_x", [128, W], F32, 3)
        ot = Rot(s.sb, "m_o", [128, W], F32, 3)
        last = (l == L - 1)
        xsrc = self.xm if l == 0 else self.xnew
        dst = self.y if last else self.xnew
        for tc in range(NC):
            tsl = slice(tc * 512, (tc + 1) * 512)
            m_t, mk = mA.next()
            P.dma('sp', m_t[:], self.mT_all[:, tsl].rearrange("(c p) n -> p c n", p=128), ['mT_all'], [mk])
            for q in range(4):
                tt = tc * 4 + q
                x_t, xk = xt.next()
                o_t, ok = ot.next()
                P.dma('sp', x_t[:], xsrc[tt * 128:(tt + 1) * 128, :], [] if l == 0 else [('xnew', tt)], [xk])
                o_ps, opk = pso.next()
                for dc in range(8):
                    P.mm(o_ps[:, 0:512], m_t[:, dc, q * 128:(q + 1) * 128], wo[:, dc, :], dc == 0, dc == 7, [mk, 'm_wo'], [opk])
                P.v('dve', 'tensor_tensor', [opk, xk], [ok], o_t[:], o_ps[:, 0:512], x_t[:], ALU.add)
                P.dma('pool', dst[tt * 128:(tt + 1) * 128, :], o_t[:], [ok], ['y_out'] if last else [('xnew', tt), ('xnew_blk', tt // 8)])
                if not last and tt % 8 == 7:
                    k = tt // 8
                    self.allgather(self.xnew[k * 1024:(k + 1) * 1024, :], self.xres[2 * k * 1024:(2 * k + 2) * 1024, :], [('xnew_blk', k)], ['xres'])
        P.barrier()
        s.close()

_CACHE = {}


def get_builder(S, L, dbg=(), stop_after=None):
    key = (S, L, tuple(dbg), stop_after)
    if key not in _CACHE:
        b = Builder(S, L, dbg, stop_after)
        b.build()
        _CACHE[key] = b
    return _CACHE[key]


def make_in_maps(b, inputs, core_ids):
    x = np.asarray(inputs["x"], dtype=np.float32)
    halves = [slice_params(inputs, h) for h in range(2)]
    maps = []
    for c in core_ids:
        bi, h = c // 2, c % 2
        m = {"x": np.ascontiguousarray(x[bi]), "xm": np.ascontiguousarray(x[bi][:, h * W:(h + 1) * W]),
             "consts": b.consts_np}
        m.update(halves[h])
        maps.append(m)
    return maps


def kernel(**inputs):
    x = np.asarray(inputs["x"])
    B, S, _ = x.shape
    L = np.asarray(inputs["norm_g"]).shape[0]
    b = get_builder(S, L)
    inputs = {k: np.asarray(v) for k, v in inputs.items()}
    n_cores = 2 * B
    maps = make_in_maps(b, inputs, list(range(n_cores)))
    res = run_bass_kernel_spmd(b.nc, maps, core_ids=list(range(n_cores)))
    out = np.empty((B, S, D), np.float32)
    for c in range(n_cores):
        out[c // 2][:, (c % 2) * W:(c % 2 + 1) * W] = res.results[c]["y"]
    return out
```
